# Optimizing a Trainium2 kernel written in Bass

```python
import math
import jax, jax.numpy as jnp
from jax import lax
import numpy as np

D_MODEL = 1024
BATCH = 16
SEQ = 2048
DEPTH = 4
DEC_BATCH = 16
DEC_SEQ = 4096
PAST_LEN = 128

MIX_WIDTH = 2 * D_MODEL
W_A = MIX_WIDTH // 2
W_B = MIX_WIDTH - W_A
CHUNK = 128
A_HEADS = 4
A_HEAD_DIM = W_A // A_HEADS
B_GROUP_CH = 16
B_GROUPS = W_B // B_GROUP_CH
B_STATE = 64
N_DIR = 2
PROJ_COLS = 3 * W_A + 2 * W_B
EPS = 1e-6
DT_MIN = 1e-3
DT_MAX = 1e-1

kernel_name = "hymba_gmlp_s5_bidir_encoder"


def rmsnorm(x, g):
    xf = x.astype(jnp.float32)
    y = xf * lax.rsqrt(jnp.mean(xf * xf, axis=-1, keepdims=True) + EPS)
    return (y * g.astype(jnp.float32)).astype(x.dtype)


def layernorm(x, g, b):
    xf = x.astype(jnp.float32)
    mu = jnp.mean(xf, axis=-1, keepdims=True)
    xc = xf - mu
    y = xc * lax.rsqrt(jnp.mean(xc * xc, axis=-1, keepdims=True) + EPS)
    return (y * g.astype(jnp.float32) + b.astype(jnp.float32)).astype(x.dtype)


def gmlp_branch(u, v, z, ln_g, ln_b, w_s, b_s):
    n, l, _ = u.shape
    v = layernorm(v, ln_g, ln_b)
    vh = v.reshape(n, l // CHUNK, CHUNK, A_HEADS, A_HEAD_DIM)
    sv = jnp.einsum('hpq,ncqhd->ncphd', w_s, vh) + b_s.T[None, None, :, :, None]
    return u * sv.reshape(n, l, W_A) * jax.nn.silu(z)


def _cmul_combine(e1, e2):
    a1r, a1i, b1r, b1i = e1
    a2r, a2i, b2r, b2i = e2
    ar = a2r * a1r - a2i * a1i
    ai = a2r * a1i + a2i * a1r
    br = a2r * b1r - a2i * b1i + b2r
    bi = a2r * b1i + a2i * b1r + b2i
    return (ar, ai, br, bi)


def s5_direction(xg, lam_re, lam_im, log_dt, b_re, b_im, c_re, c_im):
    n, l, g, c = xg.shape
    lam_re = lam_re.astype(jnp.float32)
    lam_im = lam_im.astype(jnp.float32)
    dt = jnp.exp(log_dt.astype(jnp.float32))[:, None]
    mag = jnp.exp(lam_re * dt)
    ar = mag * jnp.cos(lam_im * dt)
    ai = mag * jnp.sin(lam_im * dt)
    nr, ni = ar - 1.0, ai
    den = lam_re * lam_re + lam_im * lam_im
    fr = (nr * lam_re + ni * lam_im) / den
    fi = (ni * lam_re - nr * lam_im) / den
    b_re = b_re.astype(jnp.float32)
    b_im = b_im.astype(jnp.float32)
    bb_re = fr[..., None] * b_re - fi[..., None] * b_im
    bb_im = fr[..., None] * b_im + fi[..., None] * b_re
    c_re = c_re.astype(jnp.float32)
    c_im = c_im.astype(jnp.float32)
    t = jnp.arange(1, CHUNK + 1, dtype=jnp.float32)[:, None, None]
    pmag = jnp.exp(t * lam_re * dt)
    pr = pmag * jnp.cos(t * lam_im * dt)
    pi = pmag * jnp.sin(t * lam_im * dt)
    xc = xg.reshape(n, l // CHUNK, CHUNK, g, c).transpose(1, 0, 2, 3, 4)

    def step(h, xk):
        hr, hi = h
        bu_r = jnp.einsum('ntgc,gpc->ntgp', xk, bb_re)
        bu_i = jnp.einsum('ntgc,gpc->ntgp', xk, bb_im)
        a_r = jnp.broadcast_to(ar, bu_r.shape)
        a_i = jnp.broadcast_to(ai, bu_i.shape)
        _, _, sr, si = lax.associative_scan(_cmul_combine, (a_r, a_i, bu_r, bu_i), axis=1)
        sr = sr + pr * hr[:, None] - pi * hi[:, None]
        si = si + pr * hi[:, None] + pi * hr[:, None]
        y = jnp.einsum('ntgp,gcp->ntgc', sr, c_re) - jnp.einsum('ntgp,gcp->ntgc', si, c_im)
        return (sr[:, -1], si[:, -1]), y

    h0 = (jnp.zeros((n, g, B_STATE), jnp.float32), jnp.zeros((n, g, B_STATE), jnp.float32))
    _, y = lax.scan(step, h0, xc)
    return y.transpose(1, 0, 2, 3, 4).reshape(n, l, g, c)


def s5_branch(xb, z, lam_re, lam_im, log_dt, b_re, b_im, c_re, c_im, d_skip, w_glu, b_glu):
    n, l, _ = xb.shape
    xg = xb.astype(jnp.float32).reshape(n, l, B_GROUPS, B_GROUP_CH)
    y_fwd = s5_direction(xg, lam_re[0], lam_im[0], log_dt[0], b_re[0], b_im[0], c_re[0], c_im[0])
    y_bwd = jnp.flip(s5_direction(jnp.flip(xg, axis=1), lam_re[1], lam_im[1], log_dt[1],
                                  b_re[1], b_im[1], c_re[1], c_im[1]), axis=1)
    y = ((y_fwd + y_bwd).reshape(n, l, W_B)).astype(xb.dtype) + d_skip * xb
    gy = jax.nn.gelu(y)
    gy = gy * jax.nn.sigmoid(gy @ w_glu + b_glu)
    return gy * jax.nn.silu(z)


def trunk(x, norm_g, w_in, ln_g, ln_b, w_s, b_s, lam_re, lam_im, log_dt,
          b_re, b_im, c_re, c_im, d_skip, w_glu, b_glu, w_out, final_g):
    splits = [W_A, 2 * W_A, 3 * W_A, 3 * W_A + W_B]
    for i in range(DEPTH):
        h = rmsnorm(x, norm_g[i])
        proj = h @ w_in[i]
        u_a, v_a, z_a, x_b, z_b = jnp.split(proj, splits, axis=-1)
        y_a = gmlp_branch(jax.nn.gelu(u_a), jax.nn.gelu(v_a), z_a, ln_g[i], ln_b[i], w_s[i], b_s[i])
        y_b = s5_branch(x_b, z_b, lam_re[i], lam_im[i], log_dt[i], b_re[i], b_im[i],
                        c_re[i], c_im[i], d_skip[i], w_glu[i], b_glu[i])
        x = x + jnp.concatenate([y_a, y_b], axis=-1) @ w_out[i]
    return rmsnorm(x, final_g)


def setup_inputs(seed: int = 0) -> dict:
    key = jax.random.key(seed)
    ks = jax.random.split(key, 24)
    f32 = jnp.float32
    x_prompt = jax.random.normal(ks[0], (BATCH, SEQ, D_MODEL), f32)
    x_sample = jax.random.normal(ks[1], (DEC_BATCH, DEC_SEQ, D_MODEL), f32)
    norm_g = 1.0 + 0.05 * jax.random.normal(ks[2], (DEPTH, D_MODEL), f32)
    w_in = jax.random.normal(ks[3], (DEPTH, D_MODEL, PROJ_COLS), f32) * D_MODEL ** -0.5
    ln_g = 1.0 + 0.05 * jax.random.normal(ks[4], (DEPTH, W_A), f32)
    ln_b = 0.02 * jax.random.normal(ks[5], (DEPTH, W_A), f32)
    w_s = jax.random.normal(ks[6], (DEPTH, A_HEADS, CHUNK, CHUNK), f32) * CHUNK ** -0.5
    b_s = 1.0 + 0.1 * jax.random.normal(ks[7], (DEPTH, A_HEADS, CHUNK), f32)
    shp = (DEPTH, N_DIR, B_GROUPS, B_STATE)
    lam_re = -0.5 + 0.01 * jax.random.normal(ks[8], shp, f32)
    lam_im = math.pi * jnp.arange(B_STATE, dtype=f32) + 0.01 * jax.random.normal(ks[9], shp, f32)
    log_dt = jax.random.uniform(ks[10], (DEPTH, N_DIR, B_GROUPS), f32,
                                math.log(DT_MIN), math.log(DT_MAX))
    bshp = (DEPTH, N_DIR, B_GROUPS, B_STATE, B_GROUP_CH)
    b_re = jax.random.normal(ks[11], bshp, f32) * (2 * B_GROUP_CH) ** -0.5
    b_im = jax.random.normal(ks[12], bshp, f32) * (2 * B_GROUP_CH) ** -0.5
    cshp = (DEPTH, N_DIR, B_GROUPS, B_GROUP_CH, B_STATE)
    c_re = jax.random.normal(ks[13], cshp, f32) * (2 * B_STATE) ** -0.5
    c_im = jax.random.normal(ks[14], cshp, f32) * (2 * B_STATE) ** -0.5
    d_skip = jax.random.normal(ks[15], (DEPTH, W_B), f32)
    w_glu = jax.random.normal(ks[16], (DEPTH, W_B, W_B), f32) * W_B ** -0.5
    b_glu = 0.02 * jax.random.normal(ks[17], (DEPTH, W_B), f32)
    w_out = jax.random.normal(ks[18], (DEPTH, MIX_WIDTH, D_MODEL), f32) * MIX_WIDTH ** -0.5
    final_g = 1.0 + 0.05 * jax.random.normal(ks[19], (D_MODEL,), f32)
    return {"x_prompt": x_prompt, "x_sample": x_sample, "norm_g": norm_g, "w_in": w_in,
            "ln_g": ln_g, "ln_b": ln_b, "w_s": w_s, "b_s": b_s,
            "lam_re": lam_re, "lam_im": lam_im, "log_dt": log_dt,
            "b_re": b_re, "b_im": b_im, "c_re": c_re, "c_im": c_im,
            "d_skip": d_skip, "w_glu": w_glu, "b_glu": b_glu, "w_out": w_out,
            "final_g": final_g}


def reference(x_prompt, x_sample, norm_g, w_in, ln_g, ln_b, w_s, b_s, lam_re, lam_im, log_dt,
              b_re, b_im, c_re, c_im, d_skip, w_glu, b_glu, w_out, final_g):
    y_prompt = trunk(x_prompt, norm_g, w_in, ln_g, ln_b, w_s, b_s, lam_re, lam_im, log_dt,
                     b_re, b_im, c_re, c_im, d_skip, w_glu, b_glu, w_out, final_g)
    y_sample = trunk(x_sample, norm_g, w_in, ln_g, ln_b, w_s, b_s, lam_re, lam_im, log_dt,
                     b_re, b_im, c_re, c_im, d_skip, w_glu, b_glu, w_out, final_g)
    return (y_prompt, y_sample)
```

```python
import math
from contextlib import ExitStack
import numpy as np
import concourse.bass as bass
import concourse.mybir as mybir
from concourse.bass_utils import run_bass_kernel_spmd

F32 = mybir.dt.float32
BF = mybir.dt.bfloat16
I32 = mybir.dt.int32
AF = mybir.ActivationFunctionType
ALU = mybir.AluOpType

NL = 4
D = 1024
NTOK = 12288
SEQS = [(0, 2048), (2048, 2048), (4096, 4096), (8192, 4096)]
EPS = 1e-6
TWO_PI = 2.0 * math.pi
C1 = 6.28125
C2 = TWO_PI - 6.28125
CI_ID = 0
CI_K = 128
CI_EF = 641
CI_EB = 642
CI_MF = 643
CI_MB = 771
CI_EVF = 899
CI_EVB = 915
NCONST = 931
SMW_COLS = 1152
SAME_SYNC = True


class Sched:
    ENG = (("sp", "sync"), ("act", "scalar"), ("dve", "vector"), ("pool", "gpsimd"), ("pe", "tensor"))

    def __init__(self, nc, sems, ndma):
        self.nc = nc
        self.sems = sems
        self.E = {n: {"cnt": 0, "ops": [], "seen": {}} for n, _ in self.ENG}
        self.dnames = ["d%d" % i for i in range(ndma)]
        self.dval = [0] * ndma
        self.dnext = 0
        self.buf = {}
        self.cap = None

    def _deps(self, reads, writes, eng=None):
        d = {}

        def add(tok, raw):
            if tok is None:
                return
            if not raw and tok[0] == eng:
                return
            if d.get(tok[0], 0) < tok[1]:
                d[tok[0]] = tok[1]

        for r in reads:
            st = self.buf.get(r)
            if st:
                add(st["w"], True)
        for w in writes:
            st = self.buf.get(w)
            if st:
                add(st["w"], False)
                for s, v in st["r"].items():
                    add((s, v), False)
        return d

    def _commit(self, tok, reads, writes):
        for r in reads:
            st = self.buf.setdefault(r, {"w": None, "r": {}})
            if st["r"].get(tok[0], 0) < tok[1]:
                st["r"][tok[0]] = tok[1]
        for w in writes:
            self.buf[w] = {"w": tok, "r": {}}

    def _waits(self, eng, d):
        E = self.E[eng]
        waits = []
        for s, v in d.items():
            if s == eng and (eng == "pe" or not SAME_SYNC):
                continue
            if E["seen"].get(s, 0) >= v:
                continue
            E["seen"][s] = v
            waits.append((s, v))
        return waits

    def capture_begin(self):
        self.cap = []

    def capture_end(self):
        c, self.cap = self.cap, None
        return c

    def replay_interleaved(self, la, lb):
        ia_, ib_ = 0, 0
        while ia_ < len(la) or ib_ < len(lb):
            if ia_ < len(la):
                k, args = la[ia_]
                ia_ += 1
                (self.op if k == "op" else self.dma)(*args)
            if ib_ < len(lb):
                k, args = lb[ib_]
                ib_ += 1
                (self.op if k == "op" else self.dma)(*args)

    def op(self, eng, fn, reads=(), writes=()):
        if self.cap is not None:
            self.cap.append(("op", (eng, fn, list(reads), list(writes))))
            return
        E = self.E[eng]
        waits = self._waits(eng, self._deps(reads, writes, eng))
        E["cnt"] += 1
        E["ops"].append((waits, fn, eng, 1))
        self._commit((eng, E["cnt"]), reads, writes)

    def dma(self, fn, reads=(), writes=(), eng="sp"):
        if self.cap is not None:
            self.cap.append(("dma", (fn, list(reads), list(writes), eng)))
            return
        d = self._deps(reads, writes)
        i = self.dnext
        self.dnext = (i + 1) % len(self.dnames)
        name = self.dnames[i]
        if self.dval[i] > 0:
            d[name] = max(d.get(name, 0), self.dval[i])
        waits = self._waits(eng, d)
        self.dval[i] += 16
        self.E[eng]["ops"].append((waits, fn, name, 16))
        self._commit((name, self.dval[i]), reads, writes)

    def barrier(self):
        toks = {n: self.E[n]["cnt"] for n, _ in self.ENG}
        for i, n in enumerate(self.dnames):
            toks[n] = self.dval[i]
        for n, _ in self.ENG:
            E = self.E[n]
            waits = []
            for s, v in toks.items():
                if v > 0 and E["seen"].get(s, 0) < v and s != n:
                    E["seen"][s] = v
                    waits.append((s, v))
            E["cnt"] += 1
            E["ops"].append((waits, lambda e: e.nop(), n, 1))
        self.buf = {}

    def emit(self):
        with self.nc.Block() as block:
            for name, attr in self.ENG:
                ops = self.E[name]["ops"]
                if not ops:
                    continue

                def body(e, ops=ops):
                    for waits, fn, sn, inc in ops:
                        for s, v in waits:
                            e.wait_ge(self.sems[s], v)
                        fn(e).then_inc(self.sems[sn], inc)

                getattr(block, attr)(body)
                self.E[name]["ops"] = []


def build_program(debug=None):
    IK = "ExternalOutput" if debug else "Internal"
    nc = bass.Bass("TRN2", target_bir_lowering=False)

    def din(name, shape, dt=F32):
        return nc.dram_tensor(name, shape, dt, kind="ExternalInput").ap()

    x = din("x", [NTOK, D])
    w_in = din("w_in", [NL, D, 5120])
    w_glu = din("w_glu", [NL, D, D])
    w_out = din("w_out", [NL, 2 * D, D])
    normg = din("normg", [128, 32])
    lng = din("lng", [NL, D])
    lnb = din("lnb", [NL, D])
    bglu = din("bglu", [NL, D])
    finalg = din("finalg", [1, D])
    wsT = din("wsT", [NL, 128, 512])
    bsP = din("bsP", [NL, 128, 4])
    lamA_re = din("lamA_re", [8, 4096])
    lamA_im = din("lamA_im", [8, 4096])
    ldtA = din("ldtA", [8, 4096])
    bT_re = din("bT_re", [8, 16, 4096])
    bT_im = din("bT_im", [8, 16, 4096])
    lamP_re = din("lamP_re", [8, 128, 64])
    lamP_im = din("lamP_im", [8, 128, 64])
    ldtP = din("ldtP", [8, 128, 64])
    cP_re = din("cP_re", [8, 128, 1024])
    cP_im = din("cP_im", [8, 128, 1024])
    dskA = din("dskA", [NL, 128, 64])
    consts = din("consts", [128, NCONST])
    y = nc.dram_tensor("y", [NTOK, D], F32, kind="ExternalOutput").ap()
    wi_bf = nc.dram_tensor("wi_bf", [NL, D, 5120], BF, kind=IK).ap()
    wg_bf = nc.dram_tensor("wg_bf", [NL, D, D], BF, kind=IK).ap()
    wo_bf = nc.dram_tensor("wo_bf", [NL, 2 * D, D], BF, kind=IK).ap()
    smw = nc.dram_tensor("smw", [NL, 64, 128, SMW_COLS], BF, kind=IK).ap()
    tabs = [nc.dram_tensor("tab%d" % i, [2, 64, 128, 1026], F32, kind=IK).ap() for i in range(NL)]
    xres = y

    es = ExitStack()
    with es:
        def sb(name, shape, dt):
            return es.enter_context(nc.sbuf_tensor(name, shape, dt))

        names = ["sp", "act", "dve", "pool", "pe"] + ["d%d" % i for i in range(12)]
        sems = {n: es.enter_context(nc.semaphore(n)) for n in names}
        S = Sched(nc, sems, 12)
        ps = [es.enter_context(nc.psum_tensor("ps%d" % i, [128, 512], F32)) for i in range(8)]

        cst = sb("cst", [128, NCONST], F32)
        identb = sb("identb", [128, 128], BF)
        rho = sb("rho", [128, 8 * 64], F32)
        ngc = sb("ngc", [128, 32], F32)
        ones = sb("ones", [128, 520], F32)
        onesb = sb("onesb", [128, 128], BF)

        S.dma(lambda e: e.dma_start(out=cst[:], in_=consts[:, :]), writes=["cst"])
        S.dma(lambda e: e.dma_start(out=ngc[:], in_=normg[:, :]), writes=["ngc"])
        S.op("dve", lambda e: e.tensor_copy(out=identb[:], in_=cst[:, 0:128]), reads=["cst"], writes=["identb"])
        S.op("dve", lambda e: e.memset(ones[:], 1.0), writes=["ones"])
        S.op("dve", lambda e: e.memset(onesb[:], 1.0), writes=["onesb"])
        ident = cst[:, 0:128]

        pes = ExitStack()
        with pes:
            def psb(name, shape, dt):
                return pes.enter_context(nc.sbuf_tensor(name, shape, dt))

            cin = [psb("cin%d" % i, [128, 2560], F32) for i in range(2)]
            cob = [psb("cob%d" % i, [128, 2560], BF) for i in range(2)]
            it = 0
            for l in range(NL):
                for k in range(8):
                  for hh in range(2):
                    b = it % 2
                    it += 1
                    S.dma(lambda e, l=l, k=k, b=b, hh=hh: e.dma_start(
                        out=cin[b][:], in_=w_in[l, k * 128:(k + 1) * 128, hh * 2560:(hh + 1) * 2560]),
                          writes=["cin%d" % b])
                    S.op("act", lambda e, l=l, k=k, b=b: e.activation(out=cob[b][:], in_=cin[b][:], func=AF.Copy,
                                                                        scale=ngc[:, l * 8 + k:l * 8 + k + 1]),
                         reads=["cin%d" % b, "ngc"], writes=["cob%d" % b])
                    S.dma(lambda e, l=l, k=k, b=b, hh=hh: e.dma_start(
                        out=wi_bf[l, k * 128:(k + 1) * 128, hh * 2560:(hh + 1) * 2560], in_=cob[b][:]),
                          reads=["cob%d" % b], writes=["wi_bf"])
            for (src, dst, nk) in ((w_glu, wg_bf, 8), (w_out, wo_bf, 16)):
                for l in range(NL):
                    for k0 in range(0, nk, 2):
                        b = it % 2
                        it += 1
                        cv_in = cin[b][:, 0:2048].rearrange("p (k n) -> p k n", k=2)
                        cv_out = cob[b][:, 0:2048].rearrange("p (k n) -> p k n", k=2)
                        S.dma(lambda e, l=l, k0=k0, cv_in=cv_in, src=src: e.dma_start(
                            out=cv_in, in_=src[l, k0 * 128:(k0 + 2) * 128, :].rearrange("(k f) n -> f k n", f=128)),
                            writes=["cin%d" % b])
                        S.op("dve", lambda e, b=b: e.tensor_copy(out=cob[b][:, 0:2048], in_=cin[b][:, 0:2048]),
                             reads=["cin%d" % b], writes=["cob%d" % b])
                        S.dma(lambda e, l=l, k0=k0, cv_out=cv_out, dst=dst: e.dma_start(
                            out=dst[l, k0 * 128:(k0 + 2) * 128, :].rearrange("(k f) n -> f k n", f=128), in_=cv_out),
                            reads=["cob%d" % b], writes=["wdst"])

            GH = 8
            NA = GH * 64
            fa = {n: psb("fa_" + n, [128, NA], F32) for n in
                  ("lr", "li", "dt", "br", "bi", "al", "th", "t0", "t1", "t2", "t3", "fr", "fi", "bbr", "bbi")}
            ia = psb("ia", [128, 1026], I32)
            ia2 = psb("ia2", [128, 1026], I32)
            iabuf = [ia]
            wrw = {d: psb("wrw%d" % d, [128, GH * 128], F32) for d in range(2)}
            NB = GH * 16
            fb = {n: psb("fb_" + n, [128, NB], F32) for n in ("al", "th", "t0", "t1", "pr", "pi")}
            pb = {n: psb("pb_" + n, [128, 64], F32) for n in ("lr", "li", "dt", "al", "th", "ph", "t0", "t1")}
            cpr = psb("cpr", [128, GH * 16], F32)
            cpi = psb("cpi", [128, GH * 16], F32)
            qr = psb("qr", [128, GH * 256], F32)
            qi = psb("qi", [128, GH * 256], F32)
            qt = psb("qt", [128, GH * 256], F32)
            qm = {d: psb("qm%d" % d, [128, GH * 128], F32) for d in range(2)}
            smb = psb("smb", [128, GH * SMW_COLS], BF)
            tb = {n: psb("tb_" + n, [128, 1026], F32) for n in ("a", "k", "r", "o")}
            tbo = psb("tbo", [128, 2 * 1026], F32)
            wT4 = [psb("wT4_%d" % d, [128, 512], F32) for d in range(2)]
            mt4 = [psb("mt4_%d" % d, [128, 512], F32) for d in range(2)]
            dg4 = psb("dg4", [128, 512], F32)
            dsk = psb("dsk", [128, 64], F32)
            hpic = psb("hpic", [128, 1], F32)
            phx = psb("phx", [128, 32], F32)
            S.op("dve", lambda e: e.memset(hpic[:], math.pi / 2), writes=["hpic"])

            def dv(fn, reads, writes):
                S.op("dve", fn, reads=reads, writes=writes)

            def tt(out, a, b, op, rd, wr):
                dv(lambda e: e.tensor_tensor(out=out, in0=a, in1=b, op=op), rd, wr)

            def reduce_angle(out, a, kf, rd_keys, wr_key, n):
                ib = iabuf[0]
                ikey = "ia%d" % (0 if ib is ia else 1)
                dv(lambda e: e.tensor_scalar(out=ib[:, 0:n], in0=a, scalar1=1.0 / TWO_PI, scalar2=None, op0=ALU.mult),
                   rd_keys, [ikey, "kf" + wr_key])
                dv(lambda e: e.scalar_tensor_tensor(out=out, in0=ib[:, 0:n], scalar=-C1, in1=a, op0=ALU.mult,
                                                    op1=ALU.add), [ikey] + rd_keys, [wr_key])
                dv(lambda e: e.scalar_tensor_tensor(out=out, in0=ib[:, 0:n], scalar=-C2, in1=out, op0=ALU.mult,
                                                    op1=ALU.add), [ikey, wr_key], [wr_key])

            def sincos(s_out, c_out, ang, tmp, kf, key, n):
                reduce_angle(tmp, ang, kf, [key], key + "_red", n)
                S.op("act", lambda e: e.activation(out=s_out, in_=tmp, func=AF.Sin), reads=[key + "_red"],
                     writes=[key + "_s"])
                dv(lambda e: e.scalar_tensor_tensor(out=kf, in0=tmp, scalar=-1.0, in1=tmp, op0=ALU.mult, op1=ALU.max),
                   [key + "_red"], [key + "_sh", "kf" + key + "_sh"])
                S.op("act", lambda e: e.activation(out=c_out, in_=kf, func=AF.Sin, scale=-1.0, bias=hpic[:, 0:1]),
                     reads=[key + "_sh", "hpic"], writes=[key + "_c"])

            for l in range(NL):
                S.dma(lambda e, l=l: e.dma_start(out=dsk[:], in_=dskA[l, :, :]), writes=["dsk"])
                for gh in range(64 // GH):
                    g0 = gh * GH
                    for d in range(2):
                        ld = l * 2 + d
                        iabuf[0] = ia
                        S.capture_begin()
                        for nm, src in (("lr", lamA_re), ("li", lamA_im), ("dt", ldtA)):
                            S.dma(lambda e, nm=nm, src=src, ld=ld, g0=g0: e.dma_start(
                                out=fa[nm][:], in_=src[ld:ld + 1, g0 * 64:(g0 + GH) * 64].partition_broadcast(128)),
                                writes=["fa_" + nm])
                        for nm, src in (("br", bT_re), ("bi", bT_im)):
                            for s in range(8):
                                S.dma(lambda e, nm=nm, src=src, ld=ld, g0=g0, s=s: e.dma_start(
                                    out=fa[nm][s * 16:(s + 1) * 16, :], in_=src[ld, :, g0 * 64:(g0 + GH) * 64]),
                                    writes=["fa_" + nm + str(s)])
                        bkeys = ["fa_br%d" % s for s in range(8)] + ["fa_bi%d" % s for s in range(8)]
                        A = {n: fa[n][:] for n in fa}
                        S.op("act", lambda e: e.activation(out=A["dt"], in_=A["dt"], func=AF.Exp), reads=["fa_dt"],
                             writes=["fa_dt"])
                        tt(A["al"], A["lr"], A["dt"], ALU.mult, ["fa_lr", "fa_dt"], ["fa_al"])
                        tt(A["th"], A["li"], A["dt"], ALU.mult, ["fa_li", "fa_dt"], ["fa_th"])
                        sincos(A["t1"], A["t2"], A["th"], A["t3"], A["t0"], "fa_th", NA)
                        S.op("act", lambda e: e.activation(out=A["t0"], in_=A["al"], func=AF.Exp),
                             reads=["fa_al", "kffa_th_sh"], writes=["fa_t0"])
                        tt(A["t2"], A["t2"], A["t0"], ALU.mult, ["fa_th_c", "fa_t0"], ["fa_nr"])
                        dv(lambda e: e.tensor_scalar(out=A["t2"], in0=A["t2"], scalar1=-1.0, scalar2=None, op0=ALU.add),
                           ["fa_nr"], ["fa_nr"])
                        tt(A["t1"], A["t1"], A["t0"], ALU.mult, ["fa_th_s", "fa_t0"], ["fa_ni"])
                        tt(A["t0"], A["lr"], A["lr"], ALU.mult, ["fa_lr", "fa_ni", "fa_nr"], ["fa_den"])
                        tt(A["t3"], A["li"], A["li"], ALU.mult, ["fa_li", "fa_th_sh"], ["fa_t3"])
                        tt(A["t0"], A["t0"], A["t3"], ALU.add, ["fa_den", "fa_t3"], ["fa_den"])
                        dv(lambda e: e.reciprocal(out=A["t0"], in_=A["t0"]), ["fa_den"], ["fa_den"])
                        tt(A["fr"], A["t2"], A["lr"], ALU.mult, ["fa_nr", "fa_lr"], ["fa_fr"])
                        tt(A["t3"], A["t1"], A["li"], ALU.mult, ["fa_ni", "fa_li", "fa_den"], ["fa_t3"])
                        tt(A["fr"], A["fr"], A["t3"], ALU.add, ["fa_fr", "fa_t3"], ["fa_fr"])
                        tt(A["fr"], A["fr"], A["t0"], ALU.mult, ["fa_fr", "fa_den"], ["fa_fr"])
                        tt(A["fi"], A["t1"], A["lr"], ALU.mult, ["fa_ni", "fa_lr"], ["fa_fi"])
                        tt(A["t3"], A["t2"], A["li"], ALU.mult, ["fa_nr", "fa_li", "fa_fr"], ["fa_t3"])
                        tt(A["fi"], A["fi"], A["t3"], ALU.subtract, ["fa_fi", "fa_t3"], ["fa_fi"])
                        tt(A["fi"], A["fi"], A["t0"], ALU.mult, ["fa_fi", "fa_den"], ["fa_fi"])
                        tt(A["bbr"], A["fr"], A["br"], ALU.mult, ["fa_fr"] + bkeys, ["fa_bbr"])
                        tt(A["t3"], A["fi"], A["bi"], ALU.mult, ["fa_fi", "fa_fi"] + bkeys, ["fa_t3"])
                        tt(A["bbr"], A["bbr"], A["t3"], ALU.subtract, ["fa_bbr", "fa_t3"], ["fa_bbr"])
                        tt(A["bbi"], A["fr"], A["bi"], ALU.mult, ["fa_fr"] + bkeys, ["fa_bbi"])
                        tt(A["t3"], A["fi"], A["br"], ALU.mult, ["fa_fi", "fa_bbr"] + bkeys, ["fa_t3"])
                        tt(A["bbi"], A["bbi"], A["t3"], ALU.add, ["fa_bbi", "fa_t3"], ["fa_bbi"])
                        ecol = cst[:, CI_EF + d:CI_EF + d + 1]
                        dv(lambda e, ecol=ecol: e.tensor_scalar(out=A["fr"], in0=A["th"], scalar1=ecol, scalar2=None,
                                                                op0=ALU.mult), ["fa_th", "fa_bbi", "fa_bbr", "cst"],
                           ["fa_ang"])
                        sincos(A["t1"], A["t2"], A["fr"], A["t3"], A["t0"], "fa_ang", NA)
                        S.op("act", lambda e, ecol=ecol: e.activation(out=A["t0"], in_=A["al"], func=AF.Exp, scale=ecol),
                             reads=["fa_al", "kffa_ang_sh", "cst"], writes=["fa_t0"])
                        tt(A["t1"], A["t1"], A["t0"], ALU.mult, ["fa_ang_s", "fa_t0"], ["fa_pi"])
                        tt(A["t2"], A["t2"], A["t0"], ALU.mult, ["fa_ang_c", "fa_t0"], ["fa_pr"])
                        g3 = lambda ap: ap.rearrange("p (g q) -> p g q", q=64)
                        wv = wrw[d][:].rearrange("p (g c) -> p g c", c=128)
                        WR = wv[:, :, 0:64]
                        WI = wv[:, :, 64:128]
                        tt(WR, g3(A["bbr"]), g3(A["t2"]), ALU.mult, ["fa_bbr", "fa_pr"], ["wr%d" % d])
                        tt(A["t3"], A["bbi"], A["t1"], ALU.mult, ["fa_bbi", "fa_pi", "fa_ang_sh"], ["fa_t3"])
                        tt(WR, WR, g3(A["t3"]), ALU.subtract, ["wr%d" % d, "fa_t3"], ["wr%d" % d])
                        tt(WI, g3(A["bbr"]), g3(A["t1"]), ALU.mult, ["fa_bbr", "fa_pi"], ["wi%d" % d])
                        tt(A["t3"], A["bbi"], A["t2"], ALU.mult, ["fa_bbi", "fa_pr", "wr%d" % d], ["fa_t3"])
                        tt(WI, WI, g3(A["t3"]), ALU.add, ["wi%d" % d, "fa_t3"], ["wi%d" % d])
                        smv = smb[:].rearrange("p (g c) -> p g c", c=SMW_COLS)
                        base = 128 + d * 512
                        WRv = WR
                        WIv = WI
                        dv(lambda e, base=base, WRv=WRv: e.tensor_copy(out=smv[:, :, base:base + 64], in_=WRv), ["wr%d" % d],
                           ["smb_a%d" % d])
                        dv(lambda e, base=base, WIv=WIv: e.tensor_copy(out=smv[:, :, base + 64:base + 128], in_=WIv),
                           ["wi%d" % d], ["smb_b%d" % d])
                        dv(lambda e, base=base, WIv=WIv: e.tensor_copy(out=smv[:, :, base + 128:base + 192], in_=WIv),
                           ["wi%d" % d], ["smb_c%d" % d])
                        dv(lambda e, base=base, WRv=WRv: e.tensor_scalar(out=smv[:, :, base + 192:base + 256], in0=WRv,
                                                                scalar1=-1.0, scalar2=None, op0=ALU.mult),
                           ["wr%d" % d], ["smb_d%d" % d])

                        listA = S.capture_end()
                        iabuf[0] = ia2
                        S.capture_begin()
                        for nm, src in (("lr", lamP_re), ("li", lamP_im), ("dt", ldtP)):
                            S.dma(lambda e, nm=nm, src=src, ld=ld: e.dma_start(out=pb[nm][:], in_=src[ld, :, :]),
                                  writes=["pb_" + nm])
                        S.dma(lambda e, ld=ld, g0=g0: e.dma_start(out=cpr[:], in_=cP_re[ld, :, g0 * 16:(g0 + GH) * 16]),
                              writes=["cpr"])
                        S.dma(lambda e, ld=ld, g0=g0: e.dma_start(out=cpi[:], in_=cP_im[ld, :, g0 * 16:(g0 + GH) * 16]),
                              writes=["cpi"])
                        P = {n: pb[n][:] for n in pb}
                        S.op("act", lambda e: e.activation(out=P["dt"], in_=P["dt"], func=AF.Exp), reads=["pb_dt"],
                             writes=["pb_dt"])
                        tt(P["al"], P["lr"], P["dt"], ALU.mult, ["pb_lr", "pb_dt"], ["pb_al"])
                        tt(P["th"], P["li"], P["dt"], ALU.mult, ["pb_li", "pb_dt"], ["pb_th"])
                        S.op("act", lambda e, ld=ld: e.activation(out=rho[:, ld * 64:(ld + 1) * 64], in_=P["al"],
                                                                   func=AF.Exp, scale=8.0), reads=["pb_al"],
                             writes=["rho"])
                        ev = cst[:, CI_EVF + 16 * d:CI_EVF + 16 * d + 16]
                        B = {n: fb[n][:] for n in fb}
                        alg = P["al"][:, g0:g0 + GH].unsqueeze(2).to_broadcast([128, GH, 16])
                        thg = P["th"][:, g0:g0 + GH].unsqueeze(2).to_broadcast([128, GH, 16])
                        evb = ev.unsqueeze(1).to_broadcast([128, GH, 16])
                        b3 = lambda ap: ap.rearrange("p (g t) -> p g t", t=16)
                        tt(b3(B["al"]), alg, evb, ALU.mult, ["pb_al", "cst"], ["fb_al"])
                        tt(b3(B["th"]), thg, evb, ALU.mult, ["pb_th", "cst"], ["fb_th"])
                        sincos(B["pi"], B["pr"], B["th"], B["t0"], B["t1"], "fb_th", NB)
                        S.op("act", lambda e: e.activation(out=B["t1"], in_=B["al"], func=AF.Exp),
                             reads=["fb_al", "kffb_th_sh"], writes=["fb_t1"])
                        tt(B["pi"], B["pi"], B["t1"], ALU.mult, ["fb_th_s", "fb_t1"], ["fb_pi"])
                        tt(B["pr"], B["pr"], B["t1"], ALU.mult, ["fb_th_c", "fb_t1"], ["fb_pr"])
                        q4 = lambda ap: ap.rearrange("p (g t c) -> p g t c", t=16, c=16)
                        crb = cpr[:].rearrange("p (g c) -> p g c", c=16).unsqueeze(2).to_broadcast([128, GH, 16, 16])
                        cib = cpi[:].rearrange("p (g c) -> p g c", c=16).unsqueeze(2).to_broadcast([128, GH, 16, 16])
                        prb = b3(B["pr"]).unsqueeze(3).to_broadcast([128, GH, 16, 16])
                        pib = b3(B["pi"]).unsqueeze(3).to_broadcast([128, GH, 16, 16])
                        QRa, QIa, QTa = q4(qr[:]), q4(qi[:]), q4(qt[:])
                        tt(QRa, crb, prb, ALU.mult, ["cpr", "fb_pr"], ["q_r"])
                        tt(QTa, cib, pib, ALU.mult, ["cpi", "fb_pi"], ["q_t"])
                        tt(QRa, QRa, QTa, ALU.subtract, ["q_r", "q_t"], ["q_r"])
                        tt(QIa, crb, pib, ALU.mult, ["cpr", "fb_pi"], ["q_i"])
                        tt(QTa, cib, prb, ALU.mult, ["cpi", "fb_pr", "q_r"], ["q_t"])
                        tt(QIa, QIa, QTa, ALU.add, ["q_i", "q_t"], ["q_i"])
                        qkeys_r = ["q_r"]
                        qkeys_i = ["q_i"]
                        base = 128 + d * 512
                        o1 = smv[:, :, base + 256:base + 384].rearrange("p g (t c) -> p g t c", c=16)
                        o2 = smv[:, :, base + 384:base + 512].rearrange("p g (t c) -> p g t c", c=16)
                        QR = q4(qr[:])
                        QI = q4(qi[:])
                        dv(lambda e, o1=o1: e.tensor_copy(out=o1[0:64], in_=QR[0:64, :, 0:8, :]), qkeys_r, ["smb_e%d" % d])
                        dv(lambda e, o1=o1: e.tensor_scalar(out=o1[64:128], in0=QI[64:128, :, 0:8, :], scalar1=-1.0,
                                                     scalar2=None, op0=ALU.mult), qkeys_i, ["smb_f%d" % d])
                        dv(lambda e, o2=o2: e.tensor_scalar(out=o2[0:64], in0=QI[0:64, :, 0:8, :], scalar1=-1.0,
                                                     scalar2=None, op0=ALU.mult), qkeys_i, ["smb_g%d" % d])
                        dv(lambda e, o2=o2: e.tensor_scalar(out=o2[64:128], in0=QR[64:128, :, 0:8, :], scalar1=-1.0,
                                                     scalar2=None, op0=ALU.mult), qkeys_r, ["smb_h%d" % d])
                        qmv = qm[d][:].rearrange("p (g t c) -> p g t c", t=8, c=16)
                        dv(lambda e, qmv=qmv: e.tensor_copy(out=qmv[0:64], in_=QR[0:64, :, 8:16, :]), qkeys_r, ["qm%da" % d])
                        dv(lambda e, qmv=qmv: e.tensor_scalar(out=qmv[64:128], in0=QI[64:128, :, 8:16, :], scalar1=-1.0,
                                                     scalar2=None, op0=ALU.mult), qkeys_i, ["qm%db" % d])
                        dv(lambda e: e.tensor_scalar(out=P["t0"], in0=P["th"], scalar1=8.0, scalar2=None,
                                                     op0=ALU.mult), ["pb_th"], ["pb_ph8"])
                        reduce_angle(P["ph"], P["t0"], P["t1"], ["pb_ph8"], "pb_ph", 64)
                        if gh % 2 == 1:
                            T = {n: tb[n][:] for n in tb}
                            t3 = lambda ap: ap.rearrange("p (g k) -> p g k", k=513)
                            kb = cst[:, CI_K:CI_K + 513].unsqueeze(1).to_broadcast([128, 2, 513])
                            ph4 = P["ph"].rearrange("p (m j) -> p m j", j=4)
                            px3 = phx[:].rearrange("p (m j) -> p m j", j=2)
                            dv(lambda e: e.tensor_copy(out=px3[0:64], in_=ph4[0:64, :, 0:2]), ["pb_ph"], ["phx_a"])
                            dv(lambda e: e.tensor_copy(out=px3[64:128], in_=ph4[64:128, :, 2:4]), ["pb_ph"], ["phx_b"])
                            gs = g0 - GH
                            for m in range(gs // 4, (gs + 2 * GH) // 4):
                                gb = 4 * m
                                phb = phx[:, 2 * m:2 * m + 2].unsqueeze(2).to_broadcast([128, 2, 513])
                                tt(t3(T["a"]), phb, kb, ALU.mult, ["phx_a", "phx_b", "cst"], ["tb_a"])
                                tov = tbo[:].rearrange("p (g c k) -> p g c k", c=2, k=513)
                                reduce_angle(T["r"], T["a"], T["k"], ["tb_a"], "tb_r", 1026)
                                S.op("act", lambda e, tov=tov: e.activation(out=tov[:, :, 1, :], in_=t3(T["r"]), func=AF.Sin),
                                     reads=["tb_r"], writes=["tbo_s"])
                                dv(lambda e: e.scalar_tensor_tensor(out=T["o"], in0=T["r"], scalar=-1.0, in1=T["r"], op0=ALU.mult, op1=ALU.max),
                                   ["tb_r"], ["tb_o"])
                                S.op("act", lambda e, tov=tov: e.activation(out=tov[:, :, 0, :], in_=t3(T["o"]), func=AF.Sin,
                                                                            scale=-1.0, bias=hpic[:, 0:1]),
                                     reads=["tb_o", "hpic"], writes=["tbo_c"])
                                for hsrc in range(2):
                                    for hdst in range(2):
                                        S.dma(lambda e, ld=ld, gb=gb, hsrc=hsrc, hdst=hdst: e.dma_start(
                                            out=tabs[ld // 2][ld % 2, gb + 2 * hsrc:gb + 2 * hsrc + 2,
                                                              hdst * 64:(hdst + 1) * 64, :].rearrange("g p c -> p g c"),
                                            in_=tbo[hsrc * 64:(hsrc + 1) * 64, :].rearrange("p (g c) -> p g c", c=1026)),
                                            reads=["tbo_s", "tbo_c"], writes=["tab"])
                        listB = S.capture_end()
                        S.replay_interleaved(listA, listB)
                    smv = smb[:].rearrange("p (g c) -> p g c", c=SMW_COLS)
                    for gq in range(0, GH, 4):
                        for d in range(2):
                            bk = 2 + d
                            for q4_ in range(4):
                                gl = gq + q4_
                                S.op("pe", lambda e, d=d, gl=gl, bk=bk, q4_=q4_: e.transpose(
                                    out=ps[bk][:, q4_ * 128:(q4_ + 1) * 128], in_=wrw[d][:, gl * 128:(gl + 1) * 128],
                                    identity=ident), reads=["wr%d" % d, "wi%d" % d, "cst"], writes=["ps%d" % bk])
                            S.op("act", lambda e, bk=bk, d=d: e.activation(out=wT4[d][:], in_=ps[bk][:], func=AF.Copy),
                                 reads=["ps%d" % bk], writes=["wT%d" % d])
                            for q4_ in range(4):
                                gl = gq + q4_
                                S.op("pe", lambda e, d=d, gl=gl, q4_=q4_: e.matmul(
                                    ps[4 + d][:, q4_ * 128:(q4_ + 1) * 128], lhsT=wT4[d][:, q4_ * 128:(q4_ + 1) * 128],
                                    rhs=qm[d][:, gl * 128:(gl + 1) * 128], start=True, stop=True),
                                    reads=["wT%d" % d, "qm%da" % d, "qm%db" % d], writes=["pm%d" % d])
                            mcol = CI_MF if d == 0 else CI_MB
                            mk = cst[:, mcol:mcol + 128].unsqueeze(1).to_broadcast([128, 4, 128])
                            dv(lambda e, d=d, mk=mk: e.tensor_tensor(
                                out=mt4[d][:].rearrange("p (g c) -> p g c", c=128),
                                in0=ps[4 + d][:].rearrange("p (g c) -> p g c", c=128), in1=mk, op=ALU.mult),
                               ["pm%d" % d, "cst"], ["mt%d" % d])
                        g = g0 + gq
                        idb = ident.unsqueeze(1).to_broadcast([128, 4, 128])
                        dkb = dsk[:, g:g + 4].unsqueeze(2).to_broadcast([128, 4, 128])
                        m3 = lambda ap: ap.rearrange("p (g c) -> p g c", c=128)
                        dv(lambda e, idb=idb, dkb=dkb: e.tensor_tensor(out=m3(dg4[:]), in0=idb, in1=dkb, op=ALU.mult),
                           ["cst", "dsk"], ["dg4"])
                        dv(lambda e: e.tensor_tensor(out=mt4[0][:], in0=mt4[0][:], in1=dg4[:], op=ALU.add),
                           ["mt0", "dg4"], ["mt0"])
                        dv(lambda e, gq=gq: e.tensor_tensor(out=smv[:, gq:gq + 4, 0:128], in0=m3(mt4[0][:]),
                                                            in1=m3(mt4[1][:]), op=ALU.add), ["mt0", "mt1"], ["smb_m"])
                    allk = ["smb_m"] + ["smb_%s%d" % (c, d) for c in "abcdefgh" for d in range(2)]
                    S.dma(lambda e, l=l, g0=g0: e.dma_start(out=smw[l, g0:g0 + GH, :, :].rearrange("g p c -> p g c"),
                                                            in_=smb[:].rearrange("p (g c) -> p g c", c=SMW_COLS)),
                          reads=allk, writes=["smw"])
            S.barrier()
            S.emit()

        S.buf = {}
        XT = sb("XT", [128, 64, 512], BF)
        xs = [sb("xs%d" % i, [128, 1024], F32) for i in range(2)]
        xn = [sb("xn%d" % i, [128, 1024], BF) for i in range(2)]
        hT = sb("hT", [128, 8, 1024], BF)
        V = sb("V", [128, 8, 1024], BF)
        U = sb("U", [128, 8, 1024], BF)
        YT = sb("YT", [128, 16, 1024], BF)
        wb = [sb("wb%d" % i, [128, 8, 512], BF) for i in range(3)]
        gt = sb("gt", [128, 1024], F32)
        bt = sb("bt", [128, 1024], F32)
        bglb = sb("bglb", [1, 1024], BF)
        wsb = sb("wsb", [128, 512], F32)
        wsbb = sb("wsbb", [128, 512], BF)
        bsb = sb("bsb", [128, 4], F32)
        ss = sb("ss", [128, 8], F32)
        rs = sb("rs", [128, 8], F32)
        st6 = sb("st6", [128, 96], F32)
        mv = sb("mv", [128, 16], F32)
        lnr = sb("lnr", [128, 16], F32)
        epsc = sb("epsc", [128, 1], F32)
        rsq = sb("rsq", [128, 32], F32)
        tmpb = sb("tmpb", [128, 512], BF)
        tmpc = sb("tmpc", [128, 512], BF)
        tmpb2 = [tmpb[:], tmpc[:]]
        YTf = YT[:].rearrange("p k t -> p (k t)").bitcast(F32)
        Uf = U[:].rearrange("p k t -> p (k t)").bitcast(F32)
        hTf = hT[:].rearrange("p k t -> p (k t)").bitcast(F32)
        Vb = V[:].rearrange("p k t -> p (k t)")
        tabv = [[YTf[:, (gb * 2 + d) * 1026:(gb * 2 + d + 1) * 1026] for d in range(2)] for gb in range(3)]
        t2s = [Uf[:, i * 1040:i * 1040 + 512] for i in range(3)]
        Dset = [Uf[:, i * 1040 + 520:i * 1040 + 520 + 513] for i in range(3)]
        Sset = [hTf[:, i * 1040:i * 1040 + 513] for i in range(3)]
        mset = [hTf[:, i * 1040 + 520:i * 1040 + 520 + 513] for i in range(3)]
        smwb = [Vb[:, gb * SMW_COLS:(gb + 1) * SMW_COLS] for gb in range(3)]
        cGs = [Vb[:, 3456 + i * 1024:3456 + i * 1024 + 512] for i in range(3)]
        sGs = [Vb[:, 3456 + i * 1024 + 512:3456 + i * 1024 + 1024] for i in range(3)]

        S.op("dve", lambda e: e.memset(epsc[:], EPS), writes=["epsc"])

        def psb16(bk):
            return ps[bk][:].bitcast(BF)

        wcnt = [0]

        def load_w(src_ap, nk=8):
            b = wcnt[0] % 3
            wcnt[0] += 1
            S.dma(lambda e: e.dma_start(out=wb[b][:, 0:nk, :], in_=src_ap), reads=["wsrc"], writes=["wb%d" % b])
            return b

        acc = [0]

        def next_acc():
            acc[0] ^= 1
            return acc[0]

        trb = [0]

        def next_tr():
            trb[0] ^= 1
            return 2 + trb[0]

        Vf = V[:].rearrange("p k t -> p (k t)").bitcast(F32)

        def load_x(tok0, src, xth, wkeys):
            srcv = src[tok0:tok0 + 1024, :].rearrange("(j s) f -> j s f", s=8)
            for hh in range(2):
                S.dma(lambda e, hh=hh: e.dma_start(out=xth[hh], in_=srcv[:, 4 * hh:4 * hh + 4, :]),
                      reads=["xsrc"], writes=wkeys[hh])

        VU_KEYS = [["V%d" % c for c in range(8)] + ["GY"],
                   ["U%d" % c for c in range(8)] + ["ZB", "U"] + ["ZB%d" % c for c in range(8)]]

        def load_x_vu(tok0, src):
            load_x(tok0, src, [Vf.rearrange("p (s f) -> p s f", s=4), Uf.rearrange("p (s f) -> p s f", s=4)], VU_KEYS)

        def norm_and_hT(tok0, src, tile, compute_rs):
            if compute_rs:
                xth = [YTf[:, 0:4096].rearrange("p (s f) -> p s f", s=4), YTf[:, 4096:8192].rearrange("p (s f) -> p s f", s=4)]
                xkeys = ["YTa", "YTb"]
                load_x(tok0, src, xth, [["YTa"], ["YTb"]])
            else:
                xth = [Vf.rearrange("p (s f) -> p s f", s=4), Uf.rearrange("p (s f) -> p s f", s=4)]
                xkeys = ["GY", "ZB"]
            if compute_rs:
                for s in range(8):
                    b = s % 2
                    S.op("act", lambda e, b=b, s=s: e.activation(out=xn[b][:], in_=xth[s // 4][:, s % 4, :], func=AF.Square,
                                                                 accum_out=ss[:, s:s + 1]),
                         reads=[xkeys[s // 4]], writes=["xn%d" % b, "ss%d" % s])
                S.op("act", lambda e: e.activation(out=rsq[:, tile * 8:tile * 8 + 8], in_=ss[:, 0:8], func=AF.Sqrt,
                                                   scale=1.0 / D, bias=epsc[:, 0:1]),
                     reads=["ss%d" % s for s in range(8)] + ["epsc"], writes=["rsqt%d" % tile])
                S.op("dve", lambda e: e.reciprocal(out=rsq[:, tile * 8:tile * 8 + 8], in_=rsq[:, tile * 8:tile * 8 + 8]),
                     reads=["rsqt%d" % tile], writes=["rsqt%d" % tile])
            for s in range(8):
                b = s % 2
                q = tile * 8 + s
                S.op("dve", lambda e, b=b, q=q, s=s: e.tensor_scalar(out=xn[b][:], in0=xth[s // 4][:, s % 4, :],
                                                                     scalar1=rsq[:, q:q + 1], scalar2=None, op0=ALU.mult),
                     reads=[xkeys[s // 4], "rsqt%d" % tile], writes=["xn%d" % b])
                bk = next_tr()
                for k in range(8):
                    S.op("pe", lambda e, b=b, k=k, bk=bk: e.transpose(out=psb16(bk)[:, k * 128:(k + 1) * 128],
                                                                      in_=xn[b][:, k * 128:(k + 1) * 128],
                                                                      identity=identb[:]),
                         reads=["xn%d" % b, "identb"], writes=["ps%d" % bk])
                if s % 2 == 0:
                    S.op("act", lambda e, s=s, bk=bk: e.activation(
                        out=hT[:, :, s::8], in_=psb16(bk)[:, 0:1024].rearrange("p (k j) -> p k j", k=8), func=AF.Copy),
                        reads=["ps%d" % bk], writes=["hT"])
                else:
                    S.op("dve", lambda e, s=s, bk=bk: e.tensor_copy(
                        out=hT[:, :, s::8], in_=psb16(bk)[:, 0:1024].rearrange("p (k j) -> p k j", k=8)),
                        reads=["ps%d" % bk], writes=["hT"])

        def do_layer(t0s, L, l):
            NT = L // 1024
            J = L // 8
            for _once in range(1):
                src = x if l == 0 else xres
                wi_l = wi_bf[l].rearrange("(k f) n -> f k n", f=128)
                wg_l = wg_bf[l].rearrange("(k f) n -> f k n", f=128)
                wo_l = wo_bf[l].rearrange("(k f) n -> f k n", f=128)
                S.dma(lambda e, l=l: e.dma_start(out=gt[:], in_=lng[l:l + 1, :].partition_broadcast(128)), writes=["gt"])
                S.dma(lambda e, l=l: e.dma_start(out=bt[:], in_=lnb[l:l + 1, :].partition_broadcast(128)), writes=["bt"])
                S.dma(lambda e, l=l: e.dma_start(out=xs[0][0:1, :], in_=bglu[l:l + 1, :]), writes=["xs0"])
                S.op("dve", lambda e: e.tensor_copy(out=bglb[:], in_=xs[0][0:1, :]), reads=["xs0"], writes=["bglb"])
                S.dma(lambda e, l=l: e.dma_start(out=wsb[:], in_=wsT[l, :, :]), writes=["wsb"])
                S.op("dve", lambda e: e.tensor_copy(out=wsbb[:], in_=wsb[:]), reads=["wsb"], writes=["wsbb"])
                S.dma(lambda e, l=l: e.dma_start(out=bsb[:], in_=bsP[l, :, :]), writes=["bsb"])

                for t in range(NT):
                    tok0 = t0s + t * 1024
                    norm_and_hT(tok0, src, t, True)
                    XB = U
                    XBv = U[:].rearrange("p k t -> p (k t)").rearrange("p (g s c) -> p g s c", s=8, c=16)
                    for hf in range(2):
                        wbi = load_w(wi_l[:, :, 3072 + hf * 512:3072 + (hf + 1) * 512])
                        for s in range(8):
                            a = next_acc()
                            for k in range(8):
                                S.op("pe", lambda e, k=k, s=s, a=a, wbi=wbi: e.matmul(
                                    ps[a][:], lhsT=hT[:, k, s::8], rhs=wb[wbi][:, k, :], start=(k == 0), stop=(k == 7)),
                                    reads=["hT", "wb%d" % wbi], writes=["ps%d" % a])
                            S.op("act", lambda e, s=s, a=a, hf=hf: e.activation(
                                out=XBv[:, hf * 32:(hf + 1) * 32, s, :],
                                in_=ps[a][:].rearrange("p (g c) -> p g c", c=16), func=AF.Copy),
                                reads=["ps%d" % a], writes=["U"])
                    XB2 = U[:].rearrange("p k t -> p (k t)").rearrange("p (g m) -> p g m", m=128)
                    for g8 in range(8):
                        bk = next_tr()
                        for gi in range(8):
                            g = g8 * 8 + gi
                            S.op("pe", lambda e, g=g, gi=gi, bk=bk: e.transpose(
                                out=psb16(bk)[:, gi * 128:(gi + 1) * 128], in_=XB2[:, g, :], identity=identb[:]),
                                reads=["U", "identb"], writes=["ps%d" % bk])
                        S.op("dve", lambda e, g8=g8, bk=bk, t=t: e.tensor_copy(
                            out=XT[:, g8 * 8:(g8 + 1) * 8, t * 128:(t + 1) * 128],
                            in_=psb16(bk)[:, 0:1024].rearrange("p (g j) -> p g j", g=8)),
                            reads=["ps%d" % bk], writes=["XT%d" % g8])
                S.barrier()
                for i in range(3):
                    S.op("dve", lambda e, i=i: e.memset(Dset[i][:, 0:1], 0.0), writes=["D%d" % i])

                def s_load(g):
                    gb = g % 3
                    S.dma(lambda e, g=g, gb=gb: e.dma_start(out=smwb[gb], in_=smw[l, g, :, :]), writes=["smw%d" % gb])
                    for d in range(2):
                        S.dma(lambda e, g=g, gb=gb, d=d: e.dma_start(out=tabv[gb][d], in_=tabs[l][d, g, :, :]),
                              writes=["tab%d%d" % (gb, d)])

                def s1(u):
                    g, d = divmod(u, 2)
                    i = u % 3
                    gb = g % 3
                    base = 128 + d * 512
                    XTg = XT[:, g, 0:J]
                    xk = "XTg%d" % g
                    yb = 6 + g % 2
                    if d == 0:
                        S.op("pe", lambda e: e.matmul(ps[yb][:, 0:J], lhsT=smwb[gb][:, 0:128], rhs=XTg, start=True,
                                                      stop=False), reads=["smw%d" % gb, xk], writes=["Y%d" % (g % 2)])
                    rhs = XTg if d == 0 else XTg[:, ::-1]
                    S.op("pe", lambda e: e.matmul(ps[2 * i][:, 0:J], lhsT=smwb[gb][:, base:base + 128], rhs=rhs,
                                                  start=True, stop=True), reads=["smw%d" % gb, xk], writes=["pA%d" % i])
                    S.op("pe", lambda e: e.matmul(ps[2 * i + 1][:, 0:J], lhsT=smwb[gb][:, base + 128:base + 256], rhs=rhs,
                                                  start=True, stop=True), reads=["smw%d" % gb, xk], writes=["pB%d" % i])

                def s2(u):
                    g, d = divmod(u, 2)
                    i = u % 3
                    gb = g % 3
                    cT = tabv[gb][d][:, 0:513]
                    sT = tabv[gb][d][:, 513:1026]
                    tk = "tab%d%d" % (gb, d)
                    S.op("act", lambda e: e.activation(
                        out=mset[i][:, 0:J + 1], in_=ones[:, 0:J + 1], func=AF.Copy,
                        scale=rho[:, (l * 2 + d) * 64 + g:(l * 2 + d) * 64 + g + 1]),
                        reads=["ones", "rho"], writes=["mult%d" % i])
                    S.op("dve", lambda e: e.tensor_tensor(out=Dset[i][:, 1:J + 1], in0=ps[2 * i][:, 0:J],
                                                          in1=cT[:, 1:J + 1], op=ALU.mult),
                         reads=["pA%d" % i, tk], writes=["D%d" % i])
                    S.op("dve", lambda e: e.tensor_tensor(out=t2s[i][:, 0:J], in0=ps[2 * i + 1][:, 0:J],
                                                          in1=sT[:, 1:J + 1], op=ALU.mult),
                         reads=["pB%d" % i, tk], writes=["t2_%d" % i])
                    S.op("pool", lambda e: e.tensor_tensor(out=Dset[i][:, 1:J + 1], in0=Dset[i][:, 1:J + 1],
                                                           in1=t2s[i][:, 0:J], op=ALU.add),
                         reads=["D%d" % i, "t2_%d" % i], writes=["D%d" % i])

                def s3(u):
                    i = u % 3
                    S.op("dve", lambda e: e.tensor_tensor_scan(
                        out=Sset[i][:, 0:J + 1], data0=mset[i][:, 0:J + 1], data1=Dset[i][:, 0:J + 1], initial=0.0,
                        op0=ALU.mult, op1=ALU.add), reads=["mult%d" % i, "D%d" % i], writes=["S%d" % i])

                def s4(u):
                    g, d = divmod(u, 2)
                    i = u % 3
                    gb = g % 3
                    cT = tabv[gb][d][:, 0:513]
                    sT = tabv[gb][d][:, 513:1026]
                    tk = "tab%d%d" % (gb, d)
                    cgo = cGs[i][:, 0:J] if d == 0 else cGs[i][:, 0:J][:, ::-1]
                    sgo = sGs[i][:, 0:J] if d == 0 else sGs[i][:, 0:J][:, ::-1]
                    S.op("dve", lambda e: e.tensor_tensor(out=cgo, in0=Sset[i][:, 0:J], in1=cT[:, 0:J], op=ALU.mult),
                         reads=["S%d" % i, tk], writes=["cG%d" % i])
                    S.op("pool", lambda e: e.tensor_tensor(out=sgo, in0=Sset[i][:, 0:J], in1=sT[:, 0:J], op=ALU.mult),
                         reads=["S%d" % i, tk], writes=["sG%d" % i])

                def s5(u):
                    g, d = divmod(u, 2)
                    i = u % 3
                    gb = g % 3
                    base = 128 + d * 512
                    yb = 6 + g % 2
                    S.op("pe", lambda e: e.matmul(ps[yb][:, 0:J], lhsT=smwb[gb][:, base + 256:base + 384],
                                                  rhs=cGs[i][:, 0:J], start=False, stop=False),
                         reads=["smw%d" % gb, "cG%d" % i], writes=["Y%d" % (g % 2)])
                    S.op("pe", lambda e: e.matmul(ps[yb][:, 0:J], lhsT=smwb[gb][:, base + 384:base + 512],
                                                  rhs=sGs[i][:, 0:J], start=False, stop=(d == 1)),
                         reads=["smw%d" % gb, "sG%d" % i], writes=["Y%d" % (g % 2)])
                    if d == 1:
                        S.op("act", lambda e: e.activation(out=XT[:, g, 0:J], in_=ps[yb][:, 0:J], func=AF.Copy),
                             reads=["Y%d" % (g % 2)], writes=["XTg%d" % g])

                s_load(0)
                for step in range(128 + 2):
                    if step < 128:
                        if step % 2 == 0 and step // 2 + 1 < 64:
                            s_load(step // 2 + 1)
                        s1(step)
                        s2(step)
                    if 1 <= step <= 128:
                        s3(step - 1)
                        s4(step - 1)
                    if step >= 2:
                        s5(step - 2)
                S.barrier()
                load_x_vu(t0s, src)
                for t in range(NT):
                    tok0 = t0s + t * 1024
                    norm_and_hT(tok0, src, t, False)
                    def ln_section():
                        for c in range(8):
                            vk = "V%d" % c
                            for h2 in range(2):
                                S.op("dve", lambda e, c=c, h2=h2: e.bn_stats(out=st6[:, c * 12 + h2 * 6:c * 12 + (h2 + 1) * 6],
                                                                             in_=V[:, c, h2 * 512:(h2 + 1) * 512]),
                                     reads=[vk], writes=["st6_%d_%d" % (c, h2)])
                            S.op("dve", lambda e, c=c: e.bn_aggr(out=mv[:, c * 2:c * 2 + 2], in_=st6[:, c * 12:c * 12 + 12]),
                                 reads=["st6_%d_0" % c, "st6_%d_1" % c], writes=["mv%d" % c])
                        mvk = ["mv%d" % c for c in range(8)]
                        mvv = mv[:].rearrange("p (c t) -> p c t", t=2)
                        S.op("act", lambda e: e.activation(out=lnr[:, 0:8], in_=mvv[:, :, 1], func=AF.Sqrt, bias=epsc[:, 0:1]),
                             reads=mvk + ["epsc"], writes=["lnr_s"])
                        S.op("dve", lambda e: e.reciprocal(out=lnr[:, 0:8], in_=lnr[:, 0:8]), reads=["lnr_s"],
                             writes=["lnr_r"])
                        S.op("dve", lambda e: e.scalar_tensor_tensor(out=lnr[:, 8:16], in0=mvv[:, :, 0], scalar=-1.0,
                                                                     in1=lnr[:, 0:8], op0=ALU.mult, op1=ALU.mult),
                             reads=mvk + ["lnr_r"], writes=["lnr_b"])
                        for c in range(8):
                            vk = "V%d" % c
                            S.op("act", lambda e, c=c: e.activation(out=V[:, c, :], in_=V[:, c, :], func=AF.Identity,
                                                                    scale=lnr[:, c:c + 1], bias=lnr[:, 8 + c:9 + c]),
                                 reads=[vk, "lnr_r", "lnr_b"], writes=[vk])
                            S.op("pool", lambda e, c=c: e.tensor_tensor(out=V[:, c, :], in0=V[:, c, :], in1=gt[:],
                                                                        op=ALU.mult), reads=[vk, "gt"], writes=[vk])
                            S.op("pool", lambda e, c=c: e.tensor_tensor(out=V[:, c, :], in0=V[:, c, :], in1=bt[:],
                                                                        op=ALU.add), reads=[vk, "bt"], writes=[vk])


                    for bi_, (c0, kind) in enumerate(((1024, "v"), (1536, "v"), (0, "u"), (512, "u"), (2048, "z"), (2560, "z"))):
                        if bi_ == 2:
                            ln_section()
                        wbi = load_w(wi_l[:, :, c0:c0 + 512])
                        cc = c0 % 1024
                        for c in range(8):
                            a = next_acc()
                            for k in range(8):
                                S.op("pe", lambda e, k=k, c=c, a=a, wbi=wbi: e.matmul(
                                    ps[a][:], lhsT=hT[:, k, c * 128:(c + 1) * 128], rhs=wb[wbi][:, k, :],
                                    start=(k == 0), stop=(k == 7)), reads=["hT", "wb%d" % wbi], writes=["ps%d" % a])
                            if kind == "v":
                                S.op("act", lambda e, c=c, a=a, cc=cc: e.activation(
                                    out=V[:, c, cc:cc + 512], in_=ps[a][:], func=AF.Gelu_apprx_tanh),
                                    reads=["ps%d" % a], writes=["V%d" % c])
                            elif kind == "u":
                                S.op("act", lambda e, c=c, a=a, cc=cc: e.activation(
                                    out=U[:, c, cc:cc + 512], in_=ps[a][:], func=AF.Gelu_apprx_tanh),
                                    reads=["ps%d" % a], writes=["U%d" % c])
                            else:
                                S.op("act", lambda e, a=a: e.activation(out=tmpb[:], in_=ps[a][:], func=AF.Silu),
                                     reads=["ps%d" % a], writes=["tmpb"])
                                S.op("pool", lambda e, c=c, cc=cc: e.tensor_tensor(
                                    out=U[:, c, cc:cc + 512], in0=U[:, c, cc:cc + 512], in1=tmpb[:], op=ALU.mult),
                                    reads=["tmpb", "U%d" % c], writes=["U%d" % c])
                    def mix(c):
                        vk = "V%d" % c
                        for h in range(4):
                            bk = 4 + 2 * (c % 2) + (h // 2)
                            S.op("pe", lambda e, c=c, h=h, bk=bk: e.matmul(
                                ps[bk][:, (h % 2) * 256:(h % 2) * 256 + 256], lhsT=wsbb[:, h * 128:(h + 1) * 128],
                                rhs=V[:, c, h * 256:(h + 1) * 256], start=True, stop=True),
                                reads=["wsbb", vk], writes=["ps%d_%d" % (bk, h % 2)])
                            S.op("dve", lambda e, c=c, h=h, bk=bk: e.scalar_tensor_tensor(
                                out=U[:, c, h * 256:(h + 1) * 256], in0=ps[bk][:, (h % 2) * 256:(h % 2) * 256 + 256],
                                scalar=bsb[:, h:h + 1], in1=U[:, c, h * 256:(h + 1) * 256], op0=ALU.add, op1=ALU.mult),
                                reads=["ps%d_%d" % (bk, h % 2), "bsb", "U%d" % c], writes=["U%d" % c])

                    def yat(c):
                        bk = next_tr()
                        for k in range(8):
                            S.op("pe", lambda e, c=c, k=k, bk=bk: e.transpose(
                                out=psb16(bk)[:, k * 128:(k + 1) * 128], in_=U[:, c, k * 128:(k + 1) * 128],
                                identity=identb[:]), reads=["U%d" % c, "identb"], writes=["ps%d" % bk])
                        if c % 2 == 0:
                            S.op("act", lambda e, c=c, bk=bk: e.activation(
                                out=YT[:, 0:8, c * 128:(c + 1) * 128],
                                in_=psb16(bk)[:, 0:1024].rearrange("p (k j) -> p k j", k=8), func=AF.Copy),
                                reads=["ps%d" % bk], writes=["YTa"])
                        else:
                            S.op("dve", lambda e, c=c, bk=bk: e.tensor_copy(
                                out=YT[:, 0:8, c * 128:(c + 1) * 128],
                                in_=psb16(bk)[:, 0:1024].rearrange("p (k j) -> p k j", k=8)),
                                reads=["ps%d" % bk], writes=["YTa"])

                    for c in range(9):
                        if c < 8:
                            mix(c)
                        if c >= 1:
                            yat(c - 1)
                    ukeys = ["U%d" % c for c in range(8)]
                    vkeys = ["V%d" % c for c in range(8)]
                    ZB = U[:].rearrange("p k t -> p (k t)").rearrange("p (s f) -> p s f", s=8)
                    GY = V[:].rearrange("p k t -> p (k t)").rearrange("p (s f) -> p s f", s=8)
                    for hf in range(2):
                        wbi = load_w(wi_l[:, :, 4096 + hf * 512:4096 + (hf + 1) * 512])
                        for s in range(8):
                            a = next_acc()
                            for k in range(8):
                                S.op("pe", lambda e, k=k, s=s, a=a, wbi=wbi: e.matmul(
                                    ps[a][:], lhsT=hT[:, k, s::8], rhs=wb[wbi][:, k, :], start=(k == 0), stop=(k == 7)),
                                    reads=["hT", "wb%d" % wbi], writes=["ps%d" % a])
                            S.op("act", lambda e, s=s, a=a, hf=hf: e.activation(
                                out=ZB[:, s, hf * 512:(hf + 1) * 512], in_=ps[a][:], func=AF.Silu),
                                reads=["ps%d" % a] + ukeys, writes=["ZB"])
                    GY4 = V[:].rearrange("p k t -> p (k t)").rearrange("p (s g c) -> p g s c", s=8, c=16)
                    for g8 in range(8):
                        bk = next_tr()
                        for gi in range(8):
                            g = g8 * 8 + gi
                            S.op("pe", lambda e, g=g, gi=gi, bk=bk, t=t: e.transpose(
                                out=psb16(bk)[:, gi * 128:(gi + 1) * 128], in_=XT[:, g, t * 128:(t + 1) * 128],
                                identity=identb[:]), reads=["XT%d" % g8, "identb"], writes=["ps%d" % bk])
                        S.op("act", lambda e, g8=g8, bk=bk: e.activation(
                            out=GY4[:, g8 * 8:(g8 + 1) * 8, :, :],
                            in_=psb16(bk)[:, 0:1024].rearrange("p (g s c) -> p g s c", g=8, c=16),
                            func=AF.Gelu_apprx_tanh), reads=["ps%d" % bk] + vkeys, writes=["GY"])
                    for s in range(8):
                        bk = next_tr()
                        for k in range(8):
                            S.op("pe", lambda e, s=s, k=k, bk=bk: e.transpose(
                                out=psb16(bk)[:, k * 128:(k + 1) * 128], in_=GY[:, s, k * 128:(k + 1) * 128],
                                identity=identb[:]), reads=["GY", "identb"], writes=["ps%d" % bk])
                        if s % 2 == 0:
                            S.op("act", lambda e, s=s, bk=bk: e.activation(
                                out=hT[:, :, s::8], in_=psb16(bk)[:, 0:1024].rearrange("p (k j) -> p k j", k=8),
                                func=AF.Copy), reads=["ps%d" % bk, "ZB"], writes=["hT"])
                        else:
                            S.op("dve", lambda e, s=s, bk=bk: e.tensor_copy(
                                out=hT[:, :, s::8], in_=psb16(bk)[:, 0:1024].rearrange("p (k j) -> p k j", k=8)),
                                reads=["ps%d" % bk, "ZB"], writes=["hT"])
                    wgb = [load_w(wg_l[:, :, hf * 512:(hf + 1) * 512]) for hf in range(2)]

                    def glu(s):
                        for hf in range(2):
                            wbi = wgb[hf]
                            a = next_acc()
                            for k in range(8):
                                S.op("pe", lambda e, k=k, s=s, a=a, wbi=wbi: e.matmul(
                                    ps[a][:], lhsT=hT[:, k, s::8], rhs=wb[wbi][:, k, :], start=(k == 0), stop=False),
                                    reads=["hT", "wb%d" % wbi], writes=["ps%d" % a])
                            S.op("pe", lambda e, a=a, hf=hf: e.matmul(
                                ps[a][:], lhsT=onesb[0:1, :], rhs=bglb[0:1, hf * 512:(hf + 1) * 512], start=False,
                                stop=True), reads=["onesb", "bglb"], writes=["ps%d" % a])
                            tb_ = tmpb2[hf]
                            S.op("act", lambda e, a=a, tb_=tb_: e.activation(out=tb_, in_=ps[a][:], func=AF.Sigmoid),
                                 reads=["ps%d" % a], writes=["tmpb%d" % hf])
                            S.op("dve", lambda e, s=s, hf=hf, tb_=tb_: e.tensor_tensor(
                                out=tb_, in0=tb_, in1=GY[:, s, hf * 512:(hf + 1) * 512], op=ALU.mult),
                                reads=["tmpb%d" % hf, "GY"], writes=["tmpb%d" % hf])
                            S.op("pool", lambda e, s=s, hf=hf, tb_=tb_: e.tensor_tensor(
                                out=ZB[:, s, hf * 512:(hf + 1) * 512], in0=ZB[:, s, hf * 512:(hf + 1) * 512],
                                in1=tb_, op=ALU.mult), reads=["tmpb%d" % hf, "ZB"], writes=["ZB%d" % s])

                    def ybt(s):
                        bk = next_tr()
                        for k in range(8):
                            S.op("pe", lambda e, s=s, k=k, bk=bk: e.transpose(
                                out=psb16(bk)[:, k * 128:(k + 1) * 128], in_=ZB[:, s, k * 128:(k + 1) * 128],
                                identity=identb[:]), reads=["ZB%d" % s, "identb"], writes=["ps%d" % bk])
                        if s % 2 == 0:
                            S.op("act", lambda e, s=s, bk=bk: e.activation(
                                out=YT[:, 8:16, s::8], in_=psb16(bk)[:, 0:1024].rearrange("p (k j) -> p k j", k=8),
                                func=AF.Copy), reads=["ps%d" % bk], writes=["YTb"])
                        else:
                            S.op("dve", lambda e, s=s, bk=bk: e.tensor_copy(
                                out=YT[:, 8:16, s::8], in_=psb16(bk)[:, 0:1024].rearrange("p (k j) -> p k j", k=8)),
                                reads=["ps%d" % bk], writes=["YTb"])

                    for s in range(9):
                        if s < 8:
                            glu(s)
                        if s >= 1:
                            ybt(s - 1)
                    if t + 1 < NT:
                        load_x_vu(tok0 + 1024, src)
                    xq = [xs[i // 2][:, (i % 2) * 512:(i % 2 + 1) * 512] for i in range(4)]
                    groups = [(hf, s) for hf in range(2) for s in range(8)]

                    def rows_of(base, hf, s):
                        return base[tok0:tok0 + 1024, hf * 512:(hf + 1) * 512].rearrange("(j s) f -> j s f", s=8)[:, s, :]

                    def ld(gi):
                        hf, s = groups[gi]
                        b = gi % 4
                        rows = rows_of(src, hf, s)
                        S.dma(lambda e, b=b, rows=rows: e.dma_start(out=xq[b], in_=rows), reads=["xsrc"],
                              writes=["xq%d" % b])

                    wbo = {}
                    for gi in range(3):
                        ld(gi)
                    for gi, (hf, s) in enumerate(groups):
                        if s == 0:
                            wbo[hf] = (load_w(wo_l[:, 0:8, hf * 512:(hf + 1) * 512]),
                                       load_w(wo_l[:, 8:16, hf * 512:(hf + 1) * 512]))
                        wb0, wb1 = wbo[hf]
                        b = gi % 4
                        orows = rows_of(xres, hf, s)
                        a = next_acc()
                        for k in range(16):
                            wbi = wb0 if k < 8 else wb1
                            S.op("pe", lambda e, k=k, s=s, a=a, wbi=wbi: e.matmul(
                                ps[a][:], lhsT=YT[:, k, s::8], rhs=wb[wbi][:, k % 8, :], start=(k == 0),
                                stop=(k == 15)), reads=["YTa", "YTb", "wb%d" % wbi], writes=["ps%d" % a])
                        S.op("dve", lambda e, b=b, a=a: e.tensor_tensor(out=xq[b], in0=ps[a][:], in1=xq[b], op=ALU.add),
                             reads=["ps%d" % a, "xq%d" % b], writes=["xq%d" % b])
                        S.dma(lambda e, b=b, orows=orows: e.dma_start(out=orows, in_=xq[b]),
                              reads=["xq%d" % b], writes=["xdst"])
                        if gi + 3 < 16:
                            ld(gi + 3)
                S.barrier()
        def do_final(t0s, L):
            S.dma(lambda e: e.dma_start(out=gt[:], in_=finalg[0:1, :].partition_broadcast(128)), writes=["gt"])
            for t in range(L // 1024):
                tok0 = t0s + t * 1024
                for s in range(8):
                    b = s % 2
                    rows = xres[tok0:tok0 + 1024, :].rearrange("(j s) f -> j s f", s=8)[:, s, :]
                    orows = y[tok0:tok0 + 1024, :].rearrange("(j s) f -> j s f", s=8)[:, s, :]
                    S.dma(lambda e, b=b, rows=rows: e.dma_start(out=xs[b][:], in_=rows), writes=["xs%d" % b])
                    S.op("act", lambda e, b=b, s=s: e.activation(out=xn[b][:], in_=xs[b][:], func=AF.Square,
                                                                 accum_out=ss[:, s:s + 1]),
                         reads=["xs%d" % b], writes=["xn%d" % b, "ss%d" % s])
                    S.op("dve", lambda e, s=s: e.tensor_scalar(out=rs[:, s:s + 1], in0=ss[:, s:s + 1], scalar1=1.0 / D,
                                                               scalar2=EPS, op0=ALU.mult, op1=ALU.add),
                         reads=["ss%d" % s], writes=["rs%d" % s])
                    S.op("act", lambda e, s=s: e.activation(out=rs[:, s:s + 1], in_=rs[:, s:s + 1], func=AF.Sqrt),
                         reads=["rs%d" % s], writes=["rs%d" % s])
                    S.op("dve", lambda e, s=s: e.reciprocal(out=rs[:, s:s + 1], in_=rs[:, s:s + 1]),
                         reads=["rs%d" % s], writes=["rs%d" % s])
                    S.op("dve", lambda e, b=b, s=s: e.scalar_tensor_tensor(
                        out=xs[b][:], in0=xs[b][:], scalar=rs[:, s:s + 1], in1=gt[:], op0=ALU.mult, op1=ALU.mult),
                        reads=["xs%d" % b, "rs%d" % s, "gt"], writes=["xs%d" % b])
                    S.dma(lambda e, b=b, orows=orows: e.dma_start(out=orows, in_=xs[b][:]), reads=["xs%d" % b],
                          writes=["ydst"])
            S.barrier()

        if debug == "prologue":
            pass
        elif debug == "layer0":
            do_layer(0, 2048, 0)
        elif debug == "l0s2":
            do_layer(4096, 4096, 0)
        elif debug == "seq0":
            for l in range(NL):
                do_layer(0, 2048, l)
            do_final(0, 2048)
        else:
            for (t0s, L) in SEQS:
                for l in range(NL):
                    do_layer(t0s, L, l)
                do_final(t0s, L)
        S.barrier()
        S.emit()
    return nc


_CACHE = {}


def _consts():
    c = np.zeros((128, NCONST), np.float32)
    c[:, 0:128] = np.eye(128, dtype=np.float32)
    c[:, CI_K:CI_K + 513] = np.arange(513, dtype=np.float32)[None, :]
    sidx = np.arange(128) // 16
    c[:, CI_EF] = 7 - sidx
    c[:, CI_EB] = sidx
    tidx = (np.arange(128) // 16)[None, :]
    c[:, CI_MF:CI_MF + 128] = (tidx >= sidx[:, None]).astype(np.float32)
    c[:, CI_MB:CI_MB + 128] = (tidx <= sidx[:, None]).astype(np.float32)
    t8 = np.arange(8, dtype=np.float32)
    c[:, CI_EVF:CI_EVF + 16] = np.concatenate([t8 + 1, t8 - 7])[None, :]
    c[:, CI_EVB:CI_EVB + 16] = np.concatenate([8 - t8, -t8])[None, :]
    return c


def make_inputs(x_prompt, x_sample, norm_g, w_in, ln_g, ln_b, w_s, b_s, lam_re, lam_im, log_dt,
           b_re, b_im, c_re, c_im, d_skip, w_glu, b_glu, w_out, final_g):
    f = lambda a: np.ascontiguousarray(np.asarray(a, dtype=np.float32))
    x_prompt, x_sample = f(x_prompt), f(x_sample)
    shared = {
        "w_in": f(w_in), "w_glu": f(w_glu), "w_out": f(w_out),
        "normg": f(np.asarray(norm_g).reshape(NL, 8, 128).transpose(2, 0, 1).reshape(128, 32)),
        "lng": f(ln_g), "lnb": f(ln_b), "bglu": f(b_glu), "finalg": f(np.asarray(final_g).reshape(1, D)),
        "wsT": f(np.asarray(w_s).transpose(0, 3, 1, 2).reshape(NL, 128, 512)),
        "bsP": f(np.asarray(b_s).transpose(0, 2, 1)),
        "lamA_re": f(np.asarray(lam_re).reshape(8, 4096)),
        "lamA_im": f(np.asarray(lam_im).reshape(8, 4096)),
        "ldtA": f(np.repeat(np.asarray(log_dt).reshape(8, 64, 1), 64, axis=2).reshape(8, 4096)),
        "bT_re": f(np.asarray(b_re).reshape(8, 64, 64, 16).transpose(0, 3, 1, 2).reshape(8, 16, 4096)),
        "bT_im": f(np.asarray(b_im).reshape(8, 64, 64, 16).transpose(0, 3, 1, 2).reshape(8, 16, 4096)),
        "lamP_re": f(np.tile(np.asarray(lam_re).reshape(8, 64, 64).transpose(0, 2, 1), (1, 2, 1))),
        "lamP_im": f(np.tile(np.asarray(lam_im).reshape(8, 64, 64).transpose(0, 2, 1), (1, 2, 1))),
        "ldtP": f(np.repeat(np.asarray(log_dt).reshape(8, 1, 64), 128, axis=1)),
        "cP_re": f(np.tile(np.asarray(c_re).reshape(8, 64, 16, 64).transpose(0, 3, 1, 2).reshape(8, 64, 1024), (1, 2, 1))),
        "cP_im": f(np.tile(np.asarray(c_im).reshape(8, 64, 16, 64).transpose(0, 3, 1, 2).reshape(8, 64, 1024), (1, 2, 1))),
        "dskA": f(np.tile(np.asarray(d_skip).reshape(NL, 64, 16).transpose(0, 2, 1), (1, 8, 1))),
        "consts": _consts(),
    }
    in_maps = []
    for c in range(8):
        xc = np.concatenate([x_prompt[2 * c], x_prompt[2 * c + 1], x_sample[2 * c], x_sample[2 * c + 1]], axis=0)
        m = dict(shared)
        m["x"] = np.ascontiguousarray(xc)
        in_maps.append(m)
    return in_maps


def kernel(**inputs):
    in_maps = make_inputs(**inputs)
    if "nc" not in _CACHE:
        _CACHE["nc"] = build_program()
    nc = _CACHE["nc"]
    res = run_bass_kernel_spmd(nc, in_maps, core_ids=list(range(8)))
    yp = np.empty((16, 2048, D), np.float32)
    ysm = np.empty((16, 4096, D), np.float32)
    for c in range(8):
        yc = np.asarray(res.results[c]["y"])
        yp[2 * c] = yc[0:2048]
        yp[2 * c + 1] = yc[2048:4096]
        ysm[2 * c] = yc[4096:8192]
        ysm[2 * c + 1] = yc[8192:12288]
    return (yp, ysm)
```

```python
import math
from contextlib import ExitStack
import numpy as np
import concourse.bass as bass
import concourse.mybir as mybir
from concourse.bass_utils import run_bass_kernel_spmd

F32 = mybir.dt.float32
BF = mybir.dt.bfloat16
I32 = mybir.dt.int32
AF = mybir.ActivationFunctionType
ALU = mybir.AluOpType

NL = 4
D = 1024
NTOK = 12288
SEQS = [(0, 2048), (2048, 2048), (4096, 4096), (8192, 4096)]
EPS = 1e-6
TWO_PI = 2.0 * math.pi
C1 = 6.28125
C2 = TWO_PI - 6.28125
CI_ID = 0
CI_K = 128
CI_EF = 641
CI_EB = 642
CI_MF = 643
CI_MB = 771
CI_EVF = 899
CI_EVB = 915
NCONST = 931
SMW_COLS = 1152
SAME_SYNC = True


class Sched:
    ENG = (("sp", "sync"), ("act", "scalar"), ("dve", "vector"), ("pool", "gpsimd"), ("pe", "tensor"))

    def __init__(self, nc, sems, ndma):
        self.nc = nc
        self.sems = sems
        self.E = {n: {"cnt": 0, "ops": [], "seen": {}} for n, _ in self.ENG}
        self.dnames = ["d%d" % i for i in range(ndma)]
        self.dval = [0] * ndma
        self.dnext = 0
        self.buf = {}
        self.cap = None

    def _deps(self, reads, writes, eng=None):
        d = {}

        def add(tok, raw):
            if tok is None:
                return
            if not raw and tok[0] == eng:
                return
            if d.get(tok[0], 0) < tok[1]:
                d[tok[0]] = tok[1]

        for r in reads:
            st = self.buf.get(r)
            if st:
                add(st["w"], True)
        for w in writes:
            st = self.buf.get(w)
            if st:
                add(st["w"], False)
                for s, v in st["r"].items():
                    add((s, v), False)
        return d

    def _commit(self, tok, reads, writes):
        for r in reads:
            st = self.buf.setdefault(r, {"w": None, "r": {}})
            if st["r"].get(tok[0], 0) < tok[1]:
                st["r"][tok[0]] = tok[1]
        for w in writes:
            self.buf[w] = {"w": tok, "r": {}}

    def _waits(self, eng, d):
        E = self.E[eng]
        waits = []
        for s, v in d.items():
            if s == eng and (eng == "pe" or not SAME_SYNC):
                continue
            if E["seen"].get(s, 0) >= v:
                continue
            E["seen"][s] = v
            waits.append((s, v))
        return waits

    def capture_begin(self):
        self.cap = []

    def capture_end(self):
        c, self.cap = self.cap, None
        return c

    def replay_interleaved(self, la, lb):
        ia_, ib_ = 0, 0
        while ia_ < len(la) or ib_ < len(lb):
            if ia_ < len(la):
                k, args = la[ia_]
                ia_ += 1
                (self.op if k == "op" else self.dma)(*args)
            if ib_ < len(lb):
                k, args = lb[ib_]
                ib_ += 1
                (self.op if k == "op" else self.dma)(*args)

    def op(self, eng, fn, reads=(), writes=()):
        if self.cap is not None:
            self.cap.append(("op", (eng, fn, list(reads), list(writes))))
            return
        E = self.E[eng]
        waits = self._waits(eng, self._deps(reads, writes, eng))
        E["cnt"] += 1
        E["ops"].append((waits, fn, eng, 1))
        self._commit((eng, E["cnt"]), reads, writes)

    def dma(self, fn, reads=(), writes=(), eng="sp"):
        if self.cap is not None:
            self.cap.append(("dma", (fn, list(reads), list(writes), eng)))
            return
        d = self._deps(reads, writes)
        i = self.dnext
        self.dnext = (i + 1) % len(self.dnames)
        name = self.dnames[i]
        if self.dval[i] > 0:
            d[name] = max(d.get(name, 0), self.dval[i])
        waits = self._waits(eng, d)
        self.dval[i] += 16
        self.E[eng]["ops"].append((waits, fn, name, 16))
        self._commit((name, self.dval[i]), reads, writes)

    def barrier(self):
        toks = {n: self.E[n]["cnt"] for n, _ in self.ENG}
        for i, n in enumerate(self.dnames):
            toks[n] = self.dval[i]
        for n, _ in self.ENG:
            E = self.E[n]
            waits = []
            for s, v in toks.items():
                if v > 0 and E["seen"].get(s, 0) < v and s != n:
                    E["seen"][s] = v
                    waits.append((s, v))
            E["cnt"] += 1
            E["ops"].append((waits, lambda e: e.nop(), n, 1))
        self.buf = {}

    def emit(self):
        with self.nc.Block() as block:
            for name, attr in self.ENG:
                ops = self.E[name]["ops"]
                if not ops:
                    continue

                def body(e, ops=ops):
                    for waits, fn, sn, inc in ops:
                        for s, v in waits:
                            e.wait_ge(self.sems[s], v)
                        fn(e).then_inc(self.sems[sn], inc)

                getattr(block, attr)(body)
                self.E[name]["ops"] = []


def build_program(debug=None):
    IK = "ExternalOutput" if debug else "Internal"
    nc = bass.Bass("TRN2", target_bir_lowering=False)

    def din(name, shape, dt=F32):
        return nc.dram_tensor(name, shape, dt, kind="ExternalInput").ap()

    x = din("x", [NTOK, D])
    w_in = din("w_in", [NL, D, 5120])
    w_glu = din("w_glu", [NL, D, D])
    w_out = din("w_out", [NL, 2 * D, D])
    normg = din("normg", [128, 32])
    lng = din("lng", [NL, D])
    lnb = din("lnb", [NL, D])
    bglu = din("bglu", [NL, D])
    finalg = din("finalg", [1, D])
    wsT = din("wsT", [NL, 128, 512])
    bsP = din("bsP", [NL, 128, 4])
    lamA_re = din("lamA_re", [8, 4096])
    lamA_im = din("lamA_im", [8, 4096])
    ldtA = din("ldtA", [8, 4096])
    bT_re = din("bT_re", [8, 16, 4096])
    bT_im = din("bT_im", [8, 16, 4096])
    lamP_re = din("lamP_re", [8, 128, 64])
    lamP_im = din("lamP_im", [8, 128, 64])
    ldtP = din("ldtP", [8, 128, 64])
    cP_re = din("cP_re", [8, 128, 1024])
    cP_im = din("cP_im", [8, 128, 1024])
    dskA = din("dskA", [NL, 128, 64])
    consts = din("consts", [128, NCONST])
    y = nc.dram_tensor("y", [NTOK, D], F32, kind="ExternalOutput").ap()
    wi_bf = nc.dram_tensor("wi_bf", [NL, D, 5120], BF, kind=IK).ap()
    wg_bf = nc.dram_tensor("wg_bf", [NL, D, D], BF, kind=IK).ap()
    wo_bf = nc.dram_tensor("wo_bf", [NL, 2 * D, D], BF, kind=IK).ap()
    smw = nc.dram_tensor("smw", [NL, 64, 128, SMW_COLS], BF, kind=IK).ap()
    tabs = [nc.dram_tensor("tab%d" % i, [2, 64, 128, 1026], F32, kind=IK).ap() for i in range(NL)]
    xres = y

    es = ExitStack()
    with es:
        def sb(name, shape, dt):
            return es.enter_context(nc.sbuf_tensor(name, shape, dt))

        names = ["sp", "act", "dve", "pool", "pe"] + ["d%d" % i for i in range(12)]
        sems = {n: es.enter_context(nc.semaphore(n)) for n in names}
        S = Sched(nc, sems, 12)
        ps = [es.enter_context(nc.psum_tensor("ps%d" % i, [128, 512], F32)) for i in range(8)]

        cst = sb("cst", [128, NCONST], F32)
        identb = sb("identb", [128, 128], BF)
        rho = sb("rho", [128, 8 * 64], F32)
        ngc = sb("ngc", [128, 32], F32)
        ones = sb("ones", [128, 520], F32)
        onesb = sb("onesb", [128, 128], BF)

        S.dma(lambda e: e.dma_start(out=cst[:], in_=consts[:, :]), writes=["cst"])
        S.dma(lambda e: e.dma_start(out=ngc[:], in_=normg[:, :]), writes=["ngc"])
        S.op("dve", lambda e: e.tensor_copy(out=identb[:], in_=cst[:, 0:128]), reads=["cst"], writes=["identb"])
        S.op("dve", lambda e: e.memset(ones[:], 1.0), writes=["ones"])
        S.op("dve", lambda e: e.memset(onesb[:], 1.0), writes=["onesb"])
        ident = cst[:, 0:128]

        pes = ExitStack()
        with pes:
            def psb(name, shape, dt):
                return pes.enter_context(nc.sbuf_tensor(name, shape, dt))

            cin = [psb("cin%d" % i, [128, 2560], F32) for i in range(2)]
            cob = [psb("cob%d" % i, [128, 2560], BF) for i in range(2)]
            def conv_jobs():
                it = 0
                for l in range(NL):
                    for k in range(8):
                      for hh in range(2):
                        b = it % 2
                        it += 1
                        S.dma(lambda e, l=l, k=k, b=b, hh=hh: e.dma_start(
                            out=cin[b][:], in_=w_in[l, k * 128:(k + 1) * 128, hh * 2560:(hh + 1) * 2560]),
                              writes=["cin%d" % b])
                        S.op("act", lambda e, l=l, k=k, b=b: e.activation(out=cob[b][:], in_=cin[b][:], func=AF.Copy,
                                                                            scale=ngc[:, l * 8 + k:l * 8 + k + 1]),
                             reads=["cin%d" % b, "ngc"], writes=["cob%d" % b])
                        S.dma(lambda e, l=l, k=k, b=b, hh=hh: e.dma_start(
                            out=wi_bf[l, k * 128:(k + 1) * 128, hh * 2560:(hh + 1) * 2560], in_=cob[b][:]),
                              reads=["cob%d" % b], writes=["wi_bf"])
                        yield
                for (src, dst, nk) in ((w_glu, wg_bf, 8), (w_out, wo_bf, 16)):
                    for l in range(NL):
                        for k0 in range(0, nk, 2):
                            b = it % 2
                            it += 1
                            cv_in = cin[b][:, 0:2048].rearrange("p (k n) -> p k n", k=2)
                            cv_out = cob[b][:, 0:2048].rearrange("p (k n) -> p k n", k=2)
                            S.dma(lambda e, l=l, k0=k0, cv_in=cv_in, src=src: e.dma_start(
                                out=cv_in, in_=src[l, k0 * 128:(k0 + 2) * 128, :].rearrange("(k f) n -> f k n", f=128)),
                                writes=["cin%d" % b])
                            S.op("dve", lambda e, b=b: e.tensor_copy(out=cob[b][:, 0:2048], in_=cin[b][:, 0:2048]),
                                 reads=["cin%d" % b], writes=["cob%d" % b])
                            S.dma(lambda e, l=l, k0=k0, cv_out=cv_out, dst=dst: e.dma_start(
                                out=dst[l, k0 * 128:(k0 + 2) * 128, :].rearrange("(k f) n -> f k n", f=128), in_=cv_out),
                                reads=["cob%d" % b], writes=["wdst"])
                            yield


            jobs = conv_jobs()
            GH = 8
            NA = GH * 64
            fa = {n: psb("fa_" + n, [128, NA], F32) for n in
                  ("lr", "li", "dt", "br", "bi", "al", "th", "t0", "t1", "t2", "t3", "fr", "fi", "bbr", "bbi")}
            ia = psb("ia", [128, 1026], I32)
            ia2 = psb("ia2", [128, 1026], I32)
            iabuf = [ia]
            wrw = {d: psb("wrw%d" % d, [128, GH * 128], F32) for d in range(2)}
            NB = GH * 16
            fb = {n: psb("fb_" + n, [128, NB], F32) for n in ("al", "th", "t0", "t1", "pr", "pi")}
            pb = {n: psb("pb_" + n, [128, 64], F32) for n in ("lr", "li", "dt", "al", "th", "ph", "t0", "t1")}
            cpr = psb("cpr", [128, GH * 16], F32)
            cpi = psb("cpi", [128, GH * 16], F32)
            qr = psb("qr", [128, GH * 256], F32)
            qi = psb("qi", [128, GH * 256], F32)
            qt = psb("qt", [128, GH * 256], F32)
            qm = {d: psb("qm%d" % d, [128, GH * 128], F32) for d in range(2)}
            smb = psb("smb", [128, GH * SMW_COLS], BF)
            tb = {n: psb("tb_" + n, [128, 1026], F32) for n in ("a", "k", "r", "o")}
            tbo = psb("tbo", [128, 2 * 1026], F32)
            wT4 = [psb("wT4_%d" % d, [128, 512], F32) for d in range(2)]
            mt4 = [psb("mt4_%d" % d, [128, 512], F32) for d in range(2)]
            dg4 = psb("dg4", [128, 512], F32)
            dsk = psb("dsk", [128, 64], F32)
            hpic = psb("hpic", [128, 1], F32)
            phx = psb("phx", [128, 32], F32)
            S.op("dve", lambda e: e.memset(hpic[:], math.pi / 2), writes=["hpic"])

            def dv(fn, reads, writes):
                S.op("dve", fn, reads=reads, writes=writes)

            def tt(out, a, b, op, rd, wr):
                dv(lambda e: e.tensor_tensor(out=out, in0=a, in1=b, op=op), rd, wr)

            def reduce_angle(out, a, kf, rd_keys, wr_key, n):
                ib = iabuf[0]
                ikey = "ia%d" % (0 if ib is ia else 1)
                dv(lambda e: e.tensor_scalar(out=ib[:, 0:n], in0=a, scalar1=1.0 / TWO_PI, scalar2=None, op0=ALU.mult),
                   rd_keys, [ikey, "kf" + wr_key])
                dv(lambda e: e.scalar_tensor_tensor(out=out, in0=ib[:, 0:n], scalar=-C1, in1=a, op0=ALU.mult,
                                                    op1=ALU.add), [ikey] + rd_keys, [wr_key])
                dv(lambda e: e.scalar_tensor_tensor(out=out, in0=ib[:, 0:n], scalar=-C2, in1=out, op0=ALU.mult,
                                                    op1=ALU.add), [ikey, wr_key], [wr_key])

            def sincos(s_out, c_out, ang, tmp, kf, key, n):
                reduce_angle(tmp, ang, kf, [key], key + "_red", n)
                S.op("act", lambda e: e.activation(out=s_out, in_=tmp, func=AF.Sin), reads=[key + "_red"],
                     writes=[key + "_s"])
                dv(lambda e: e.scalar_tensor_tensor(out=kf, in0=tmp, scalar=-1.0, in1=tmp, op0=ALU.mult, op1=ALU.max),
                   [key + "_red"], [key + "_sh", "kf" + key + "_sh"])
                S.op("act", lambda e: e.activation(out=c_out, in_=kf, func=AF.Sin, scale=-1.0, bias=hpic[:, 0:1]),
                     reads=[key + "_sh", "hpic"], writes=[key + "_c"])

            for l in range(NL):
                S.dma(lambda e, l=l: e.dma_start(out=dsk[:], in_=dskA[l, :, :]), writes=["dsk"])
                for gh in range(64 // GH):
                    g0 = gh * GH
                    for d in range(2):
                        ld = l * 2 + d
                        for _j in range(2):
                            next(jobs, None)
                        iabuf[0] = ia
                        S.capture_begin()
                        for nm, src in (("lr", lamA_re), ("li", lamA_im), ("dt", ldtA)):
                            S.dma(lambda e, nm=nm, src=src, ld=ld, g0=g0: e.dma_start(
                                out=fa[nm][:], in_=src[ld:ld + 1, g0 * 64:(g0 + GH) * 64].partition_broadcast(128)),
                                writes=["fa_" + nm])
                        for nm, src in (("br", bT_re), ("bi", bT_im)):
                            for s in range(8):
                                S.dma(lambda e, nm=nm, src=src, ld=ld, g0=g0, s=s: e.dma_start(
                                    out=fa[nm][s * 16:(s + 1) * 16, :], in_=src[ld, :, g0 * 64:(g0 + GH) * 64]),
                                    writes=["fa_" + nm + str(s)])
                        bkeys = ["fa_br%d" % s for s in range(8)] + ["fa_bi%d" % s for s in range(8)]
                        A = {n: fa[n][:] for n in fa}
                        S.op("act", lambda e: e.activation(out=A["dt"], in_=A["dt"], func=AF.Exp), reads=["fa_dt"],
                             writes=["fa_dt"])
                        tt(A["al"], A["lr"], A["dt"], ALU.mult, ["fa_lr", "fa_dt"], ["fa_al"])
                        tt(A["th"], A["li"], A["dt"], ALU.mult, ["fa_li", "fa_dt"], ["fa_th"])
                        sincos(A["t1"], A["t2"], A["th"], A["t3"], A["t0"], "fa_th", NA)
                        S.op("act", lambda e: e.activation(out=A["t0"], in_=A["al"], func=AF.Exp),
                             reads=["fa_al", "kffa_th_sh"], writes=["fa_t0"])
                        tt(A["t2"], A["t2"], A["t0"], ALU.mult, ["fa_th_c", "fa_t0"], ["fa_nr"])
                        dv(lambda e: e.tensor_scalar(out=A["t2"], in0=A["t2"], scalar1=-1.0, scalar2=None, op0=ALU.add),
                           ["fa_nr"], ["fa_nr"])
                        tt(A["t1"], A["t1"], A["t0"], ALU.mult, ["fa_th_s", "fa_t0"], ["fa_ni"])
                        tt(A["t0"], A["lr"], A["lr"], ALU.mult, ["fa_lr", "fa_ni", "fa_nr"], ["fa_den"])
                        tt(A["t3"], A["li"], A["li"], ALU.mult, ["fa_li", "fa_th_sh"], ["fa_t3"])
                        tt(A["t0"], A["t0"], A["t3"], ALU.add, ["fa_den", "fa_t3"], ["fa_den"])
                        dv(lambda e: e.reciprocal(out=A["t0"], in_=A["t0"]), ["fa_den"], ["fa_den"])
                        tt(A["fr"], A["t2"], A["lr"], ALU.mult, ["fa_nr", "fa_lr"], ["fa_fr"])
                        tt(A["t3"], A["t1"], A["li"], ALU.mult, ["fa_ni", "fa_li", "fa_den"], ["fa_t3"])
                        tt(A["fr"], A["fr"], A["t3"], ALU.add, ["fa_fr", "fa_t3"], ["fa_fr"])
                        tt(A["fr"], A["fr"], A["t0"], ALU.mult, ["fa_fr", "fa_den"], ["fa_fr"])
                        tt(A["fi"], A["t1"], A["lr"], ALU.mult, ["fa_ni", "fa_lr"], ["fa_fi"])
                        tt(A["t3"], A["t2"], A["li"], ALU.mult, ["fa_nr", "fa_li", "fa_fr"], ["fa_t3"])
                        tt(A["fi"], A["fi"], A["t3"], ALU.subtract, ["fa_fi", "fa_t3"], ["fa_fi"])
                        tt(A["fi"], A["fi"], A["t0"], ALU.mult, ["fa_fi", "fa_den"], ["fa_fi"])
                        tt(A["bbr"], A["fr"], A["br"], ALU.mult, ["fa_fr"] + bkeys, ["fa_bbr"])
                        tt(A["t3"], A["fi"], A["bi"], ALU.mult, ["fa_fi", "fa_fi"] + bkeys, ["fa_t3"])
                        tt(A["bbr"], A["bbr"], A["t3"], ALU.subtract, ["fa_bbr", "fa_t3"], ["fa_bbr"])
                        tt(A["bbi"], A["fr"], A["bi"], ALU.mult, ["fa_fr"] + bkeys, ["fa_bbi"])
                        tt(A["t3"], A["fi"], A["br"], ALU.mult, ["fa_fi", "fa_bbr"] + bkeys, ["fa_t3"])
                        tt(A["bbi"], A["bbi"], A["t3"], ALU.add, ["fa_bbi", "fa_t3"], ["fa_bbi"])
                        ecol = cst[:, CI_EF + d:CI_EF + d + 1]
                        dv(lambda e, ecol=ecol: e.tensor_scalar(out=A["fr"], in0=A["th"], scalar1=ecol, scalar2=None,
                                                                op0=ALU.mult), ["fa_th", "fa_bbi", "fa_bbr", "cst"],
                           ["fa_ang"])
                        sincos(A["t1"], A["t2"], A["fr"], A["t3"], A["t0"], "fa_ang", NA)
                        S.op("act", lambda e, ecol=ecol: e.activation(out=A["t0"], in_=A["al"], func=AF.Exp, scale=ecol),
                             reads=["fa_al", "kffa_ang_sh", "cst"], writes=["fa_t0"])
                        tt(A["t1"], A["t1"], A["t0"], ALU.mult, ["fa_ang_s", "fa_t0"], ["fa_pi"])
                        tt(A["t2"], A["t2"], A["t0"], ALU.mult, ["fa_ang_c", "fa_t0"], ["fa_pr"])
                        g3 = lambda ap: ap.rearrange("p (g q) -> p g q", q=64)
                        wv = wrw[d][:].rearrange("p (g c) -> p g c", c=128)
                        WR = wv[:, :, 0:64]
                        WI = wv[:, :, 64:128]
                        tt(WR, g3(A["bbr"]), g3(A["t2"]), ALU.mult, ["fa_bbr", "fa_pr"], ["wr%d" % d])
                        tt(A["t3"], A["bbi"], A["t1"], ALU.mult, ["fa_bbi", "fa_pi", "fa_ang_sh"], ["fa_t3"])
                        tt(WR, WR, g3(A["t3"]), ALU.subtract, ["wr%d" % d, "fa_t3"], ["wr%d" % d])
                        tt(WI, g3(A["bbr"]), g3(A["t1"]), ALU.mult, ["fa_bbr", "fa_pi"], ["wi%d" % d])
                        tt(A["t3"], A["bbi"], A["t2"], ALU.mult, ["fa_bbi", "fa_pr", "wr%d" % d], ["fa_t3"])
                        tt(WI, WI, g3(A["t3"]), ALU.add, ["wi%d" % d, "fa_t3"], ["wi%d" % d])
                        smv = smb[:].rearrange("p (g c) -> p g c", c=SMW_COLS)
                        base = 128 + d * 512
                        WRv = WR
                        WIv = WI
                        dv(lambda e, base=base, WRv=WRv: e.tensor_copy(out=smv[:, :, base:base + 64], in_=WRv), ["wr%d" % d],
                           ["smb_a%d" % d])
                        dv(lambda e, base=base, WIv=WIv: e.tensor_copy(out=smv[:, :, base + 64:base + 128], in_=WIv),
                           ["wi%d" % d], ["smb_b%d" % d])
                        dv(lambda e, base=base, WIv=WIv: e.tensor_copy(out=smv[:, :, base + 128:base + 192], in_=WIv),
                           ["wi%d" % d], ["smb_c%d" % d])
                        dv(lambda e, base=base, WRv=WRv: e.tensor_scalar(out=smv[:, :, base + 192:base + 256], in0=WRv,
                                                                scalar1=-1.0, scalar2=None, op0=ALU.mult),
                           ["wr%d" % d], ["smb_d%d" % d])

                        listA = S.capture_end()
                        iabuf[0] = ia2
                        S.capture_begin()
                        for nm, src in (("lr", lamP_re), ("li", lamP_im), ("dt", ldtP)):
                            S.dma(lambda e, nm=nm, src=src, ld=ld: e.dma_start(out=pb[nm][:], in_=src[ld, :, :]),
                                  writes=["pb_" + nm])
                        S.dma(lambda e, ld=ld, g0=g0: e.dma_start(out=cpr[:], in_=cP_re[ld, :, g0 * 16:(g0 + GH) * 16]),
                              writes=["cpr"])
                        S.dma(lambda e, ld=ld, g0=g0: e.dma_start(out=cpi[:], in_=cP_im[ld, :, g0 * 16:(g0 + GH) * 16]),
                              writes=["cpi"])
                        P = {n: pb[n][:] for n in pb}
                        S.op("act", lambda e: e.activation(out=P["dt"], in_=P["dt"], func=AF.Exp), reads=["pb_dt"],
                             writes=["pb_dt"])
                        tt(P["al"], P["lr"], P["dt"], ALU.mult, ["pb_lr", "pb_dt"], ["pb_al"])
                        tt(P["th"], P["li"], P["dt"], ALU.mult, ["pb_li", "pb_dt"], ["pb_th"])
                        S.op("act", lambda e, ld=ld: e.activation(out=rho[:, ld * 64:(ld + 1) * 64], in_=P["al"],
                                                                   func=AF.Exp, scale=8.0), reads=["pb_al"],
                             writes=["rho"])
                        ev = cst[:, CI_EVF + 16 * d:CI_EVF + 16 * d + 16]
                        B = {n: fb[n][:] for n in fb}
                        alg = P["al"][:, g0:g0 + GH].unsqueeze(2).to_broadcast([128, GH, 16])
                        thg = P["th"][:, g0:g0 + GH].unsqueeze(2).to_broadcast([128, GH, 16])
                        evb = ev.unsqueeze(1).to_broadcast([128, GH, 16])
                        b3 = lambda ap: ap.rearrange("p (g t) -> p g t", t=16)
                        tt(b3(B["al"]), alg, evb, ALU.mult, ["pb_al", "cst"], ["fb_al"])
                        tt(b3(B["th"]), thg, evb, ALU.mult, ["pb_th", "cst"], ["fb_th"])
                        sincos(B["pi"], B["pr"], B["th"], B["t0"], B["t1"], "fb_th", NB)
                        S.op("act", lambda e: e.activation(out=B["t1"], in_=B["al"], func=AF.Exp),
                             reads=["fb_al", "kffb_th_sh"], writes=["fb_t1"])
                        tt(B["pi"], B["pi"], B["t1"], ALU.mult, ["fb_th_s", "fb_t1"], ["fb_pi"])
                        tt(B["pr"], B["pr"], B["t1"], ALU.mult, ["fb_th_c", "fb_t1"], ["fb_pr"])
                        q4 = lambda ap: ap.rearrange("p (g t c) -> p g t c", t=16, c=16)
                        crb = cpr[:].rearrange("p (g c) -> p g c", c=16).unsqueeze(2).to_broadcast([128, GH, 16, 16])
                        cib = cpi[:].rearrange("p (g c) -> p g c", c=16).unsqueeze(2).to_broadcast([128, GH, 16, 16])
                        prb = b3(B["pr"]).unsqueeze(3).to_broadcast([128, GH, 16, 16])
                        pib = b3(B["pi"]).unsqueeze(3).to_broadcast([128, GH, 16, 16])
                        QRa, QIa, QTa = q4(qr[:]), q4(qi[:]), q4(qt[:])
                        tt(QRa, crb, prb, ALU.mult, ["cpr", "fb_pr"], ["q_r"])
                        tt(QTa, cib, pib, ALU.mult, ["cpi", "fb_pi"], ["q_t"])
                        tt(QRa, QRa, QTa, ALU.subtract, ["q_r", "q_t"], ["q_r"])
                        tt(QIa, crb, pib, ALU.mult, ["cpr", "fb_pi"], ["q_i"])
                        tt(QTa, cib, prb, ALU.mult, ["cpi", "fb_pr", "q_r"], ["q_t"])
                        tt(QIa, QIa, QTa, ALU.add, ["q_i", "q_t"], ["q_i"])
                        qkeys_r = ["q_r"]
                        qkeys_i = ["q_i"]
                        base = 128 + d * 512
                        o1 = smv[:, :, base + 256:base + 384].rearrange("p g (t c) -> p g t c", c=16)
                        o2 = smv[:, :, base + 384:base + 512].rearrange("p g (t c) -> p g t c", c=16)
                        QR = q4(qr[:])
                        QI = q4(qi[:])
                        dv(lambda e, o1=o1: e.tensor_copy(out=o1[0:64], in_=QR[0:64, :, 0:8, :]), qkeys_r, ["smb_e%d" % d])
                        dv(lambda e, o1=o1: e.tensor_scalar(out=o1[64:128], in0=QI[64:128, :, 0:8, :], scalar1=-1.0,
                                                     scalar2=None, op0=ALU.mult), qkeys_i, ["smb_f%d" % d])
                        dv(lambda e, o2=o2: e.tensor_scalar(out=o2[0:64], in0=QI[0:64, :, 0:8, :], scalar1=-1.0,
                                                     scalar2=None, op0=ALU.mult), qkeys_i, ["smb_g%d" % d])
                        dv(lambda e, o2=o2: e.tensor_scalar(out=o2[64:128], in0=QR[64:128, :, 0:8, :], scalar1=-1.0,
                                                     scalar2=None, op0=ALU.mult), qkeys_r, ["smb_h%d" % d])
                        qmv = qm[d][:].rearrange("p (g t c) -> p g t c", t=8, c=16)
                        dv(lambda e, qmv=qmv: e.tensor_copy(out=qmv[0:64], in_=QR[0:64, :, 8:16, :]), qkeys_r, ["qm%da" % d])
                        dv(lambda e, qmv=qmv: e.tensor_scalar(out=qmv[64:128], in0=QI[64:128, :, 8:16, :], scalar1=-1.0,
                                                     scalar2=None, op0=ALU.mult), qkeys_i, ["qm%db" % d])
                        dv(lambda e: e.tensor_scalar(out=P["t0"], in0=P["th"], scalar1=8.0, scalar2=None,
                                                     op0=ALU.mult), ["pb_th"], ["pb_ph8"])
                        reduce_angle(P["ph"], P["t0"], P["t1"], ["pb_ph8"], "pb_ph", 64)
                        if gh % 2 == 1:
                            T = {n: tb[n][:] for n in tb}
                            t3 = lambda ap: ap.rearrange("p (g k) -> p g k", k=513)
                            kb = cst[:, CI_K:CI_K + 513].unsqueeze(1).to_broadcast([128, 2, 513])
                            ph4 = P["ph"].rearrange("p (m j) -> p m j", j=4)
                            px3 = phx[:].rearrange("p (m j) -> p m j", j=2)
                            dv(lambda e: e.tensor_copy(out=px3[0:64], in_=ph4[0:64, :, 0:2]), ["pb_ph"], ["phx_a"])
                            dv(lambda e: e.tensor_copy(out=px3[64:128], in_=ph4[64:128, :, 2:4]), ["pb_ph"], ["phx_b"])
                            gs = g0 - GH
                            for m in range(gs // 4, (gs + 2 * GH) // 4):
                                gb = 4 * m
                                phb = phx[:, 2 * m:2 * m + 2].unsqueeze(2).to_broadcast([128, 2, 513])
                                tt(t3(T["a"]), phb, kb, ALU.mult, ["phx_a", "phx_b", "cst"], ["tb_a"])
                                tov = tbo[:].rearrange("p (g c k) -> p g c k", c=2, k=513)
                                reduce_angle(T["r"], T["a"], T["k"], ["tb_a"], "tb_r", 1026)
                                S.op("act", lambda e, tov=tov: e.activation(out=tov[:, :, 1, :], in_=t3(T["r"]), func=AF.Sin),
                                     reads=["tb_r"], writes=["tbo_s"])
                                dv(lambda e: e.scalar_tensor_tensor(out=T["o"], in0=T["r"], scalar=-1.0, in1=T["r"], op0=ALU.mult, op1=ALU.max),
                                   ["tb_r"], ["tb_o"])
                                S.op("act", lambda e, tov=tov: e.activation(out=tov[:, :, 0, :], in_=t3(T["o"]), func=AF.Sin,
                                                                            scale=-1.0, bias=hpic[:, 0:1]),
                                     reads=["tb_o", "hpic"], writes=["tbo_c"])
                                for hsrc in range(2):
                                    for hdst in range(2):
                                        S.dma(lambda e, ld=ld, gb=gb, hsrc=hsrc, hdst=hdst: e.dma_start(
                                            out=tabs[ld // 2][ld % 2, gb + 2 * hsrc:gb + 2 * hsrc + 2,
                                                              hdst * 64:(hdst + 1) * 64, :].rearrange("g p c -> p g c"),
                                            in_=tbo[hsrc * 64:(hsrc + 1) * 64, :].rearrange("p (g c) -> p g c", c=1026)),
                                            reads=["tbo_s", "tbo_c"], writes=["tab"])
                        listB = S.capture_end()
                        S.replay_interleaved(listA, listB)
                    smv = smb[:].rearrange("p (g c) -> p g c", c=SMW_COLS)
                    for gq in range(0, GH, 4):
                        for d in range(2):
                            bk = 2 + d
                            for q4_ in range(4):
                                gl = gq + q4_
                                S.op("pe", lambda e, d=d, gl=gl, bk=bk, q4_=q4_: e.transpose(
                                    out=ps[bk][:, q4_ * 128:(q4_ + 1) * 128], in_=wrw[d][:, gl * 128:(gl + 1) * 128],
                                    identity=ident), reads=["wr%d" % d, "wi%d" % d, "cst"], writes=["ps%d" % bk])
                            S.op("act", lambda e, bk=bk, d=d: e.activation(out=wT4[d][:], in_=ps[bk][:], func=AF.Copy),
                                 reads=["ps%d" % bk], writes=["wT%d" % d])
                            for q4_ in range(4):
                                gl = gq + q4_
                                S.op("pe", lambda e, d=d, gl=gl, q4_=q4_: e.matmul(
                                    ps[4 + d][:, q4_ * 128:(q4_ + 1) * 128], lhsT=wT4[d][:, q4_ * 128:(q4_ + 1) * 128],
                                    rhs=qm[d][:, gl * 128:(gl + 1) * 128], start=True, stop=True),
                                    reads=["wT%d" % d, "qm%da" % d, "qm%db" % d], writes=["pm%d" % d])
                            mcol = CI_MF if d == 0 else CI_MB
                            mk = cst[:, mcol:mcol + 128].unsqueeze(1).to_broadcast([128, 4, 128])
                            dv(lambda e, d=d, mk=mk: e.tensor_tensor(
                                out=mt4[d][:].rearrange("p (g c) -> p g c", c=128),
                                in0=ps[4 + d][:].rearrange("p (g c) -> p g c", c=128), in1=mk, op=ALU.mult),
                               ["pm%d" % d, "cst"], ["mt%d" % d])
                        g = g0 + gq
                        idb = ident.unsqueeze(1).to_broadcast([128, 4, 128])
                        dkb = dsk[:, g:g + 4].unsqueeze(2).to_broadcast([128, 4, 128])
                        m3 = lambda ap: ap.rearrange("p (g c) -> p g c", c=128)
                        dv(lambda e, idb=idb, dkb=dkb: e.tensor_tensor(out=m3(dg4[:]), in0=idb, in1=dkb, op=ALU.mult),
                           ["cst", "dsk"], ["dg4"])
                        dv(lambda e: e.tensor_tensor(out=mt4[0][:], in0=mt4[0][:], in1=dg4[:], op=ALU.add),
                           ["mt0", "dg4"], ["mt0"])
                        dv(lambda e, gq=gq: e.tensor_tensor(out=smv[:, gq:gq + 4, 0:128], in0=m3(mt4[0][:]),
                                                            in1=m3(mt4[1][:]), op=ALU.add), ["mt0", "mt1"], ["smb_m"])
                    allk = ["smb_m"] + ["smb_%s%d" % (c, d) for c in "abcdefgh" for d in range(2)]
                    S.dma(lambda e, l=l, g0=g0: e.dma_start(out=smw[l, g0:g0 + GH, :, :].rearrange("g p c -> p g c"),
                                                            in_=smb[:].rearrange("p (g c) -> p g c", c=SMW_COLS)),
                          reads=allk, writes=["smw"])
            for _ in jobs:
                pass
            S.barrier()
            S.emit()

        S.buf = {}
        XT = sb("XT", [128, 64, 512], BF)
        xs = [sb("xs%d" % i, [128, 1024], F32) for i in range(2)]
        xn = [sb("xn%d" % i, [128, 1024], BF) for i in range(2)]
        hT = sb("hT", [128, 8, 1024], BF)
        V = sb("V", [128, 8, 1024], BF)
        U = sb("U", [128, 8, 1024], BF)
        YT = sb("YT", [128, 16, 1024], BF)
        wb = [sb("wb%d" % i, [128, 8, 512], BF) for i in range(3)]
        gt = sb("gt", [128, 1024], F32)
        bt = sb("bt", [128, 1024], F32)
        bglb = sb("bglb", [1, 1024], BF)
        wsb = sb("wsb", [128, 512], F32)
        wsbb = sb("wsbb", [128, 512], BF)
        bsb = sb("bsb", [128, 4], F32)
        ss = sb("ss", [128, 8], F32)
        rs = sb("rs", [128, 8], F32)
        st6 = sb("st6", [128, 96], F32)
        mv = sb("mv", [128, 16], F32)
        lnr = sb("lnr", [128, 16], F32)
        epsc = sb("epsc", [128, 1], F32)
        rsq = sb("rsq", [128, 32], F32)
        tmpb = sb("tmpb", [128, 512], BF)
        tmpc = sb("tmpc", [128, 512], BF)
        tmpb2 = [tmpb[:], tmpc[:]]
        YTf = YT[:].rearrange("p k t -> p (k t)").bitcast(F32)
        Uf = U[:].rearrange("p k t -> p (k t)").bitcast(F32)
        hTf = hT[:].rearrange("p k t -> p (k t)").bitcast(F32)
        Vb = V[:].rearrange("p k t -> p (k t)")
        tabv = [[YTf[:, (gb * 2 + d) * 1026:(gb * 2 + d + 1) * 1026] for d in range(2)] for gb in range(3)]
        t2s = [Uf[:, i * 1040:i * 1040 + 512] for i in range(3)]
        Dset = [Uf[:, i * 1040 + 520:i * 1040 + 520 + 513] for i in range(3)]
        Sset = [hTf[:, i * 1040:i * 1040 + 513] for i in range(3)]
        mset = [hTf[:, i * 1040 + 520:i * 1040 + 520 + 513] for i in range(3)]
        smwb = [Vb[:, gb * SMW_COLS:(gb + 1) * SMW_COLS] for gb in range(3)]
        cGs = [Vb[:, 3456 + i * 1024:3456 + i * 1024 + 512] for i in range(3)]
        sGs = [Vb[:, 3456 + i * 1024 + 512:3456 + i * 1024 + 1024] for i in range(3)]

        S.op("dve", lambda e: e.memset(epsc[:], EPS), writes=["epsc"])

        def psb16(bk):
            return ps[bk][:].bitcast(BF)

        wcnt = [0]

        def load_w(src_ap, nk=8):
            b = wcnt[0] % 3
            wcnt[0] += 1
            S.dma(lambda e: e.dma_start(out=wb[b][:, 0:nk, :], in_=src_ap), reads=["wsrc"], writes=["wb%d" % b])
            return b

        acc = [0]

        def next_acc():
            acc[0] ^= 1
            return acc[0]

        trb = [0]

        def next_tr():
            trb[0] ^= 1
            return 2 + trb[0]

        Vf = V[:].rearrange("p k t -> p (k t)").bitcast(F32)

        def load_x(tok0, src, xth, wkeys):
            srcv = src[tok0:tok0 + 1024, :].rearrange("(j s) f -> j s f", s=8)
            for hh in range(2):
                S.dma(lambda e, hh=hh: e.dma_start(out=xth[hh], in_=srcv[:, 4 * hh:4 * hh + 4, :]),
                      reads=["xsrc"], writes=wkeys[hh])

        VU_KEYS = [["V%d" % c for c in range(8)] + ["GY"],
                   ["U%d" % c for c in range(8)] + ["ZB", "U"] + ["ZB%d" % c for c in range(8)]]

        def load_x_vu(tok0, src):
            load_x(tok0, src, [Vf.rearrange("p (s f) -> p s f", s=4), Uf.rearrange("p (s f) -> p s f", s=4)], VU_KEYS)

        def norm_and_hT(tok0, src, tile, compute_rs):
            if compute_rs:
                xth = [YTf[:, 0:4096].rearrange("p (s f) -> p s f", s=4), YTf[:, 4096:8192].rearrange("p (s f) -> p s f", s=4)]
                xkeys = ["YTa", "YTb"]
                load_x(tok0, src, xth, [["YTa"], ["YTb"]])
            else:
                xth = [Vf.rearrange("p (s f) -> p s f", s=4), Uf.rearrange("p (s f) -> p s f", s=4)]
                xkeys = ["GY", "ZB"]
            if compute_rs:
                for s in range(8):
                    b = s % 2
                    S.op("act", lambda e, b=b, s=s: e.activation(out=xn[b][:], in_=xth[s // 4][:, s % 4, :], func=AF.Square,
                                                                 accum_out=ss[:, s:s + 1]),
                         reads=[xkeys[s // 4]], writes=["xn%d" % b, "ss%d" % s])
                S.op("act", lambda e: e.activation(out=rsq[:, tile * 8:tile * 8 + 8], in_=ss[:, 0:8], func=AF.Sqrt,
                                                   scale=1.0 / D, bias=epsc[:, 0:1]),
                     reads=["ss%d" % s for s in range(8)] + ["epsc"], writes=["rsqt%d" % tile])
                S.op("dve", lambda e: e.reciprocal(out=rsq[:, tile * 8:tile * 8 + 8], in_=rsq[:, tile * 8:tile * 8 + 8]),
                     reads=["rsqt%d" % tile], writes=["rsqt%d" % tile])
            for s in range(8):
                b = s % 2
                q = tile * 8 + s
                S.op("dve", lambda e, b=b, q=q, s=s: e.tensor_scalar(out=xn[b][:], in0=xth[s // 4][:, s % 4, :],
                                                                     scalar1=rsq[:, q:q + 1], scalar2=None, op0=ALU.mult),
                     reads=[xkeys[s // 4], "rsqt%d" % tile], writes=["xn%d" % b])
                bk = next_tr()
                for k in range(8):
                    S.op("pe", lambda e, b=b, k=k, bk=bk: e.transpose(out=psb16(bk)[:, k * 128:(k + 1) * 128],
                                                                      in_=xn[b][:, k * 128:(k + 1) * 128],
                                                                      identity=identb[:]),
                         reads=["xn%d" % b, "identb"], writes=["ps%d" % bk])
                if s % 2 == 0:
                    S.op("act", lambda e, s=s, bk=bk: e.activation(
                        out=hT[:, :, s::8], in_=psb16(bk)[:, 0:1024].rearrange("p (k j) -> p k j", k=8), func=AF.Copy),
                        reads=["ps%d" % bk], writes=["hT"])
                else:
                    S.op("dve", lambda e, s=s, bk=bk: e.tensor_copy(
                        out=hT[:, :, s::8], in_=psb16(bk)[:, 0:1024].rearrange("p (k j) -> p k j", k=8)),
                        reads=["ps%d" % bk], writes=["hT"])

        def do_layer(t0s, L, l):
            NT = L // 1024
            J = L // 8
            for _once in range(1):
                src = x if l == 0 else xres
                wi_l = wi_bf[l].rearrange("(k f) n -> f k n", f=128)
                wg_l = wg_bf[l].rearrange("(k f) n -> f k n", f=128)
                wo_l = wo_bf[l].rearrange("(k f) n -> f k n", f=128)
                S.dma(lambda e, l=l: e.dma_start(out=gt[:], in_=lng[l:l + 1, :].partition_broadcast(128)), writes=["gt"])
                S.dma(lambda e, l=l: e.dma_start(out=bt[:], in_=lnb[l:l + 1, :].partition_broadcast(128)), writes=["bt"])
                S.dma(lambda e, l=l: e.dma_start(out=xs[0][0:1, :], in_=bglu[l:l + 1, :]), writes=["xs0"])
                S.op("dve", lambda e: e.tensor_copy(out=bglb[:], in_=xs[0][0:1, :]), reads=["xs0"], writes=["bglb"])
                S.dma(lambda e, l=l: e.dma_start(out=wsb[:], in_=wsT[l, :, :]), writes=["wsb"])
                S.op("dve", lambda e: e.tensor_copy(out=wsbb[:], in_=wsb[:]), reads=["wsb"], writes=["wsbb"])
                S.dma(lambda e, l=l: e.dma_start(out=bsb[:], in_=bsP[l, :, :]), writes=["bsb"])

                for t in range(NT):
                    tok0 = t0s + t * 1024
                    norm_and_hT(tok0, src, t, True)
                    XB = U
                    XBv = U[:].rearrange("p k t -> p (k t)").rearrange("p (g s c) -> p g s c", s=8, c=16)
                    for hf in range(2):
                        wbi = load_w(wi_l[:, :, 3072 + hf * 512:3072 + (hf + 1) * 512])
                        for s in range(8):
                            a = next_acc()
                            for k in range(8):
                                S.op("pe", lambda e, k=k, s=s, a=a, wbi=wbi: e.matmul(
                                    ps[a][:], lhsT=hT[:, k, s::8], rhs=wb[wbi][:, k, :], start=(k == 0), stop=(k == 7)),
                                    reads=["hT", "wb%d" % wbi], writes=["ps%d" % a])
                            S.op("act", lambda e, s=s, a=a, hf=hf: e.activation(
                                out=XBv[:, hf * 32:(hf + 1) * 32, s, :],
                                in_=ps[a][:].rearrange("p (g c) -> p g c", c=16), func=AF.Copy),
                                reads=["ps%d" % a], writes=["U"])
                    XB2 = U[:].rearrange("p k t -> p (k t)").rearrange("p (g m) -> p g m", m=128)
                    for g8 in range(8):
                        bk = next_tr()
                        for gi in range(8):
                            g = g8 * 8 + gi
                            S.op("pe", lambda e, g=g, gi=gi, bk=bk: e.transpose(
                                out=psb16(bk)[:, gi * 128:(gi + 1) * 128], in_=XB2[:, g, :], identity=identb[:]),
                                reads=["U", "identb"], writes=["ps%d" % bk])
                        S.op("dve", lambda e, g8=g8, bk=bk, t=t: e.tensor_copy(
                            out=XT[:, g8 * 8:(g8 + 1) * 8, t * 128:(t + 1) * 128],
                            in_=psb16(bk)[:, 0:1024].rearrange("p (g j) -> p g j", g=8)),
                            reads=["ps%d" % bk], writes=["XT%d" % g8])
                S.barrier()
                for i in range(3):
                    S.op("dve", lambda e, i=i: e.memset(Dset[i][:, 0:1], 0.0), writes=["D%d" % i])

                def s_load(g):
                    gb = g % 3
                    S.dma(lambda e, g=g, gb=gb: e.dma_start(out=smwb[gb], in_=smw[l, g, :, :]), writes=["smw%d" % gb])
                    for d in range(2):
                        S.dma(lambda e, g=g, gb=gb, d=d: e.dma_start(out=tabv[gb][d], in_=tabs[l][d, g, :, :]),
                              writes=["tab%d%d" % (gb, d)])

                def s1(u):
                    g, d = divmod(u, 2)
                    i = u % 3
                    gb = g % 3
                    base = 128 + d * 512
                    XTg = XT[:, g, 0:J]
                    xk = "XTg%d" % g
                    yb = 6 + g % 2
                    if d == 0:
                        S.op("pe", lambda e: e.matmul(ps[yb][:, 0:J], lhsT=smwb[gb][:, 0:128], rhs=XTg, start=True,
                                                      stop=False), reads=["smw%d" % gb, xk], writes=["Y%d" % (g % 2)])
                    rhs = XTg if d == 0 else XTg[:, ::-1]
                    S.op("pe", lambda e: e.matmul(ps[2 * i][:, 0:J], lhsT=smwb[gb][:, base:base + 128], rhs=rhs,
                                                  start=True, stop=True), reads=["smw%d" % gb, xk], writes=["pA%d" % i])
                    S.op("pe", lambda e: e.matmul(ps[2 * i + 1][:, 0:J], lhsT=smwb[gb][:, base + 128:base + 256], rhs=rhs,
                                                  start=True, stop=True), reads=["smw%d" % gb, xk], writes=["pB%d" % i])

                def s2(u):
                    g, d = divmod(u, 2)
                    i = u % 3
                    gb = g % 3
                    cT = tabv[gb][d][:, 0:513]
                    sT = tabv[gb][d][:, 513:1026]
                    tk = "tab%d%d" % (gb, d)
                    S.op("act", lambda e: e.activation(
                        out=mset[i][:, 0:J + 1], in_=ones[:, 0:J + 1], func=AF.Copy,
                        scale=rho[:, (l * 2 + d) * 64 + g:(l * 2 + d) * 64 + g + 1]),
                        reads=["ones", "rho"], writes=["mult%d" % i])
                    S.op("dve", lambda e: e.tensor_tensor(out=Dset[i][:, 1:J + 1], in0=ps[2 * i][:, 0:J],
                                                          in1=cT[:, 1:J + 1], op=ALU.mult),
                         reads=["pA%d" % i, tk], writes=["D%d" % i])
                    S.op("dve", lambda e: e.tensor_tensor(out=t2s[i][:, 0:J], in0=ps[2 * i + 1][:, 0:J],
                                                          in1=sT[:, 1:J + 1], op=ALU.mult),
                         reads=["pB%d" % i, tk], writes=["t2_%d" % i])
                    S.op("pool", lambda e: e.tensor_tensor(out=Dset[i][:, 1:J + 1], in0=Dset[i][:, 1:J + 1],
                                                           in1=t2s[i][:, 0:J], op=ALU.add),
                         reads=["D%d" % i, "t2_%d" % i], writes=["D%d" % i])

                def s3(u):
                    i = u % 3
                    S.op("dve", lambda e: e.tensor_tensor_scan(
                        out=Sset[i][:, 0:J + 1], data0=mset[i][:, 0:J + 1], data1=Dset[i][:, 0:J + 1], initial=0.0,
                        op0=ALU.mult, op1=ALU.add), reads=["mult%d" % i, "D%d" % i], writes=["S%d" % i])

                def s4(u):
                    g, d = divmod(u, 2)
                    i = u % 3
                    gb = g % 3
                    cT = tabv[gb][d][:, 0:513]
                    sT = tabv[gb][d][:, 513:1026]
                    tk = "tab%d%d" % (gb, d)
                    cgo = cGs[i][:, 0:J] if d == 0 else cGs[i][:, 0:J][:, ::-1]
                    sgo = sGs[i][:, 0:J] if d == 0 else sGs[i][:, 0:J][:, ::-1]
                    S.op("dve", lambda e: e.tensor_tensor(out=cgo, in0=Sset[i][:, 0:J], in1=cT[:, 0:J], op=ALU.mult),
                         reads=["S%d" % i, tk], writes=["cG%d" % i])
                    S.op("pool", lambda e: e.tensor_tensor(out=sgo, in0=Sset[i][:, 0:J], in1=sT[:, 0:J], op=ALU.mult),
                         reads=["S%d" % i, tk], writes=["sG%d" % i])

                def s5(u):
                    g, d = divmod(u, 2)
                    i = u % 3
                    gb = g % 3
                    base = 128 + d * 512
                    yb = 6 + g % 2
                    S.op("pe", lambda e: e.matmul(ps[yb][:, 0:J], lhsT=smwb[gb][:, base + 256:base + 384],
                                                  rhs=cGs[i][:, 0:J], start=False, stop=False),
                         reads=["smw%d" % gb, "cG%d" % i], writes=["Y%d" % (g % 2)])
                    S.op("pe", lambda e: e.matmul(ps[yb][:, 0:J], lhsT=smwb[gb][:, base + 384:base + 512],
                                                  rhs=sGs[i][:, 0:J], start=False, stop=(d == 1)),
                         reads=["smw%d" % gb, "sG%d" % i], writes=["Y%d" % (g % 2)])
                    if d == 1:
                        S.op("act", lambda e: e.activation(out=XT[:, g, 0:J], in_=ps[yb][:, 0:J], func=AF.Copy),
                             reads=["Y%d" % (g % 2)], writes=["XTg%d" % g])

                s_load(0)
                for step in range(128 + 2):
                    if step < 128:
                        if step % 2 == 0 and step // 2 + 1 < 64:
                            s_load(step // 2 + 1)
                        s1(step)
                        s2(step)
                    if 1 <= step <= 128:
                        s3(step - 1)
                        s4(step - 1)
                    if step >= 2:
                        s5(step - 2)
                S.barrier()
                load_x_vu(t0s, src)
                for t in range(NT):
                    tok0 = t0s + t * 1024
                    norm_and_hT(tok0, src, t, False)
                    def ln_section():
                        for c in range(8):
                            vk = "V%d" % c
                            for h2 in range(2):
                                S.op("dve", lambda e, c=c, h2=h2: e.bn_stats(out=st6[:, c * 12 + h2 * 6:c * 12 + (h2 + 1) * 6],
                                                                             in_=V[:, c, h2 * 512:(h2 + 1) * 512]),
                                     reads=[vk], writes=["st6_%d_%d" % (c, h2)])
                            S.op("dve", lambda e, c=c: e.bn_aggr(out=mv[:, c * 2:c * 2 + 2], in_=st6[:, c * 12:c * 12 + 12]),
                                 reads=["st6_%d_0" % c, "st6_%d_1" % c], writes=["mv%d" % c])
                        mvk = ["mv%d" % c for c in range(8)]
                        mvv = mv[:].rearrange("p (c t) -> p c t", t=2)
                        S.op("act", lambda e: e.activation(out=lnr[:, 0:8], in_=mvv[:, :, 1], func=AF.Sqrt, bias=epsc[:, 0:1]),
                             reads=mvk + ["epsc"], writes=["lnr_s"])
                        S.op("dve", lambda e: e.reciprocal(out=lnr[:, 0:8], in_=lnr[:, 0:8]), reads=["lnr_s"],
                             writes=["lnr_r"])
                        S.op("dve", lambda e: e.scalar_tensor_tensor(out=lnr[:, 8:16], in0=mvv[:, :, 0], scalar=-1.0,
                                                                     in1=lnr[:, 0:8], op0=ALU.mult, op1=ALU.mult),
                             reads=mvk + ["lnr_r"], writes=["lnr_b"])
                        for c in range(8):
                            vk = "V%d" % c
                            S.op("act", lambda e, c=c: e.activation(out=V[:, c, :], in_=V[:, c, :], func=AF.Identity,
                                                                    scale=lnr[:, c:c + 1], bias=lnr[:, 8 + c:9 + c]),
                                 reads=[vk, "lnr_r", "lnr_b"], writes=[vk])
                            S.op("pool", lambda e, c=c: e.tensor_tensor(out=V[:, c, :], in0=V[:, c, :], in1=gt[:],
                                                                        op=ALU.mult), reads=[vk, "gt"], writes=[vk])
                            S.op("pool", lambda e, c=c: e.tensor_tensor(out=V[:, c, :], in0=V[:, c, :], in1=bt[:],
                                                                        op=ALU.add), reads=[vk, "bt"], writes=[vk])


                    for bi_, (c0, kind) in enumerate(((1024, "v"), (1536, "v"), (0, "u"), (512, "u"), (2048, "z"), (2560, "z"))):
                        if bi_ == 2:
                            ln_section()
                        wbi = load_w(wi_l[:, :, c0:c0 + 512])
                        cc = c0 % 1024
                        for c in range(8):
                            a = next_acc()
                            for k in range(8):
                                S.op("pe", lambda e, k=k, c=c, a=a, wbi=wbi: e.matmul(
                                    ps[a][:], lhsT=hT[:, k, c * 128:(c + 1) * 128], rhs=wb[wbi][:, k, :],
                                    start=(k == 0), stop=(k == 7)), reads=["hT", "wb%d" % wbi], writes=["ps%d" % a])
                            if kind == "v":
                                S.op("act", lambda e, c=c, a=a, cc=cc: e.activation(
                                    out=V[:, c, cc:cc + 512], in_=ps[a][:], func=AF.Gelu_apprx_tanh),
                                    reads=["ps%d" % a], writes=["V%d" % c])
                            elif kind == "u":
                                S.op("act", lambda e, c=c, a=a, cc=cc: e.activation(
                                    out=U[:, c, cc:cc + 512], in_=ps[a][:], func=AF.Gelu_apprx_tanh),
                                    reads=["ps%d" % a], writes=["U%d" % c])
                            else:
                                S.op("act", lambda e, a=a: e.activation(out=tmpb[:], in_=ps[a][:], func=AF.Silu),
                                     reads=["ps%d" % a], writes=["tmpb"])
                                S.op("pool", lambda e, c=c, cc=cc: e.tensor_tensor(
                                    out=U[:, c, cc:cc + 512], in0=U[:, c, cc:cc + 512], in1=tmpb[:], op=ALU.mult),
                                    reads=["tmpb", "U%d" % c], writes=["U%d" % c])
                    def mix(c):
                        vk = "V%d" % c
                        for h in range(4):
                            bk = 4 + 2 * (c % 2) + (h // 2)
                            S.op("pe", lambda e, c=c, h=h, bk=bk: e.matmul(
                                ps[bk][:, (h % 2) * 256:(h % 2) * 256 + 256], lhsT=wsbb[:, h * 128:(h + 1) * 128],
                                rhs=V[:, c, h * 256:(h + 1) * 256], start=True, stop=True),
                                reads=["wsbb", vk], writes=["ps%d_%d" % (bk, h % 2)])
                            S.op("dve", lambda e, c=c, h=h, bk=bk: e.scalar_tensor_tensor(
                                out=U[:, c, h * 256:(h + 1) * 256], in0=ps[bk][:, (h % 2) * 256:(h % 2) * 256 + 256],
                                scalar=bsb[:, h:h + 1], in1=U[:, c, h * 256:(h + 1) * 256], op0=ALU.add, op1=ALU.mult),
                                reads=["ps%d_%d" % (bk, h % 2), "bsb", "U%d" % c], writes=["U%d" % c])

                    def yat(c):
                        bk = next_tr()
                        for k in range(8):
                            S.op("pe", lambda e, c=c, k=k, bk=bk: e.transpose(
                                out=psb16(bk)[:, k * 128:(k + 1) * 128], in_=U[:, c, k * 128:(k + 1) * 128],
                                identity=identb[:]), reads=["U%d" % c, "identb"], writes=["ps%d" % bk])
                        if c % 2 == 0:
                            S.op("act", lambda e, c=c, bk=bk: e.activation(
                                out=YT[:, 0:8, c * 128:(c + 1) * 128],
                                in_=psb16(bk)[:, 0:1024].rearrange("p (k j) -> p k j", k=8), func=AF.Copy),
                                reads=["ps%d" % bk], writes=["YTa"])
                        else:
                            S.op("dve", lambda e, c=c, bk=bk: e.tensor_copy(
                                out=YT[:, 0:8, c * 128:(c + 1) * 128],
                                in_=psb16(bk)[:, 0:1024].rearrange("p (k j) -> p k j", k=8)),
                                reads=["ps%d" % bk], writes=["YTa"])

                    for c in range(9):
                        if c < 8:
                            mix(c)
                        if c >= 1:
                            yat(c - 1)
                    ukeys = ["U%d" % c for c in range(8)]
                    vkeys = ["V%d" % c for c in range(8)]
                    ZB = U[:].rearrange("p k t -> p (k t)").rearrange("p (s f) -> p s f", s=8)
                    GY = V[:].rearrange("p k t -> p (k t)").rearrange("p (s f) -> p s f", s=8)
                    for hf in range(2):
                        wbi = load_w(wi_l[:, :, 4096 + hf * 512:4096 + (hf + 1) * 512])
                        for s in range(8):
                            a = next_acc()
                            for k in range(8):
                                S.op("pe", lambda e, k=k, s=s, a=a, wbi=wbi: e.matmul(
                                    ps[a][:], lhsT=hT[:, k, s::8], rhs=wb[wbi][:, k, :], start=(k == 0), stop=(k == 7)),
                                    reads=["hT", "wb%d" % wbi], writes=["ps%d" % a])
                            S.op("act", lambda e, s=s, a=a, hf=hf: e.activation(
                                out=ZB[:, s, hf * 512:(hf + 1) * 512], in_=ps[a][:], func=AF.Silu),
                                reads=["ps%d" % a] + ukeys, writes=["ZB"])
                    GY4 = V[:].rearrange("p k t -> p (k t)").rearrange("p (s g c) -> p g s c", s=8, c=16)
                    for g8 in range(8):
                        bk = next_tr()
                        for gi in range(8):
                            g = g8 * 8 + gi
                            S.op("pe", lambda e, g=g, gi=gi, bk=bk, t=t: e.transpose(
                                out=psb16(bk)[:, gi * 128:(gi + 1) * 128], in_=XT[:, g, t * 128:(t + 1) * 128],
                                identity=identb[:]), reads=["XT%d" % g8, "identb"], writes=["ps%d" % bk])
                        S.op("act", lambda e, g8=g8, bk=bk: e.activation(
                            out=GY4[:, g8 * 8:(g8 + 1) * 8, :, :],
                            in_=psb16(bk)[:, 0:1024].rearrange("p (g s c) -> p g s c", g=8, c=16),
                            func=AF.Gelu_apprx_tanh), reads=["ps%d" % bk] + vkeys, writes=["GY"])
                    for s in range(8):
                        bk = next_tr()
                        for k in range(8):
                            S.op("pe", lambda e, s=s, k=k, bk=bk: e.transpose(
                                out=psb16(bk)[:, k * 128:(k + 1) * 128], in_=GY[:, s, k * 128:(k + 1) * 128],
                                identity=identb[:]), reads=["GY", "identb"], writes=["ps%d" % bk])
                        if s % 2 == 0:
                            S.op("act", lambda e, s=s, bk=bk: e.activation(
                                out=hT[:, :, s::8], in_=psb16(bk)[:, 0:1024].rearrange("p (k j) -> p k j", k=8),
                                func=AF.Copy), reads=["ps%d" % bk, "ZB"], writes=["hT"])
                        else:
                            S.op("dve", lambda e, s=s, bk=bk: e.tensor_copy(
                                out=hT[:, :, s::8], in_=psb16(bk)[:, 0:1024].rearrange("p (k j) -> p k j", k=8)),
                                reads=["ps%d" % bk, "ZB"], writes=["hT"])
                    wgb = [load_w(wg_l[:, :, hf * 512:(hf + 1) * 512]) for hf in range(2)]

                    def glu(s):
                        for hf in range(2):
                            wbi = wgb[hf]
                            a = next_acc()
                            for k in range(8):
                                S.op("pe", lambda e, k=k, s=s, a=a, wbi=wbi: e.matmul(
                                    ps[a][:], lhsT=hT[:, k, s::8], rhs=wb[wbi][:, k, :], start=(k == 0), stop=False),
                                    reads=["hT", "wb%d" % wbi], writes=["ps%d" % a])
                            S.op("pe", lambda e, a=a, hf=hf: e.matmul(
                                ps[a][:], lhsT=onesb[0:1, :], rhs=bglb[0:1, hf * 512:(hf + 1) * 512], start=False,
                                stop=True), reads=["onesb", "bglb"], writes=["ps%d" % a])
                            tb_ = tmpb2[hf]
                            S.op("act", lambda e, a=a, tb_=tb_: e.activation(out=tb_, in_=ps[a][:], func=AF.Sigmoid),
                                 reads=["ps%d" % a], writes=["tmpb%d" % hf])
                            S.op("dve", lambda e, s=s, hf=hf, tb_=tb_: e.tensor_tensor(
                                out=tb_, in0=tb_, in1=GY[:, s, hf * 512:(hf + 1) * 512], op=ALU.mult),
                                reads=["tmpb%d" % hf, "GY"], writes=["tmpb%d" % hf])
                            S.op("pool", lambda e, s=s, hf=hf, tb_=tb_: e.tensor_tensor(
                                out=ZB[:, s, hf * 512:(hf + 1) * 512], in0=ZB[:, s, hf * 512:(hf + 1) * 512],
                                in1=tb_, op=ALU.mult), reads=["tmpb%d" % hf, "ZB"], writes=["ZB%d" % s])

                    def ybt(s):
                        bk = next_tr()
                        for k in range(8):
                            S.op("pe", lambda e, s=s, k=k, bk=bk: e.transpose(
                                out=psb16(bk)[:, k * 128:(k + 1) * 128], in_=ZB[:, s, k * 128:(k + 1) * 128],
                                identity=identb[:]), reads=["ZB%d" % s, "identb"], writes=["ps%d" % bk])
                        if s % 2 == 0:
                            S.op("act", lambda e, s=s, bk=bk: e.activation(
                                out=YT[:, 8:16, s::8], in_=psb16(bk)[:, 0:1024].rearrange("p (k j) -> p k j", k=8),
                                func=AF.Copy), reads=["ps%d" % bk], writes=["YTb"])
                        else:
                            S.op("dve", lambda e, s=s, bk=bk: e.tensor_copy(
                                out=YT[:, 8:16, s::8], in_=psb16(bk)[:, 0:1024].rearrange("p (k j) -> p k j", k=8)),
                                reads=["ps%d" % bk], writes=["YTb"])

                    for s in range(9):
                        if s < 8:
                            glu(s)
                        if s >= 1:
                            ybt(s - 1)
                    if t + 1 < NT:
                        load_x_vu(tok0 + 1024, src)
                    xq = [xs[i // 2][:, (i % 2) * 512:(i % 2 + 1) * 512] for i in range(4)]
                    groups = [(hf, s) for hf in range(2) for s in range(8)]

                    def rows_of(base, hf, s):
                        return base[tok0:tok0 + 1024, hf * 512:(hf + 1) * 512].rearrange("(j s) f -> j s f", s=8)[:, s, :]

                    def ld(gi):
                        hf, s = groups[gi]
                        b = gi % 4
                        rows = rows_of(src, hf, s)
                        S.dma(lambda e, b=b, rows=rows: e.dma_start(out=xq[b], in_=rows), reads=["xsrc"],
                              writes=["xq%d" % b])

                    wbo = {}
                    for gi in range(3):
                        ld(gi)
                    for gi, (hf, s) in enumerate(groups):
                        if s == 0:
                            wbo[hf] = (load_w(wo_l[:, 0:8, hf * 512:(hf + 1) * 512]),
                                       load_w(wo_l[:, 8:16, hf * 512:(hf + 1) * 512]))
                        wb0, wb1 = wbo[hf]
                        b = gi % 4
                        orows = rows_of(xres, hf, s)
                        a = next_acc()
                        for k in range(16):
                            wbi = wb0 if k < 8 else wb1
                            S.op("pe", lambda e, k=k, s=s, a=a, wbi=wbi: e.matmul(
                                ps[a][:], lhsT=YT[:, k, s::8], rhs=wb[wbi][:, k % 8, :], start=(k == 0),
                                stop=(k == 15)), reads=["YTa", "YTb", "wb%d" % wbi], writes=["ps%d" % a])
                        S.op("dve", lambda e, b=b, a=a: e.tensor_tensor(out=xq[b], in0=ps[a][:], in1=xq[b], op=ALU.add),
                             reads=["ps%d" % a, "xq%d" % b], writes=["xq%d" % b])
                        S.dma(lambda e, b=b, orows=orows: e.dma_start(out=orows, in_=xq[b]),
                              reads=["xq%d" % b], writes=["xdst"])
                        if gi + 3 < 16:
                            ld(gi + 3)
                S.barrier()
        def do_final(t0s, L):
            S.dma(lambda e: e.dma_start(out=gt[:], in_=finalg[0:1, :].partition_broadcast(128)), writes=["gt"])
            for t in range(L // 1024):
                tok0 = t0s + t * 1024
                for s in range(8):
                    b = s % 2
                    rows = xres[tok0:tok0 + 1024, :].rearrange("(j s) f -> j s f", s=8)[:, s, :]
                    orows = y[tok0:tok0 + 1024, :].rearrange("(j s) f -> j s f", s=8)[:, s, :]
                    S.dma(lambda e, b=b, rows=rows: e.dma_start(out=xs[b][:], in_=rows), writes=["xs%d" % b])
                    S.op("act", lambda e, b=b, s=s: e.activation(out=xn[b][:], in_=xs[b][:], func=AF.Square,
                                                                 accum_out=ss[:, s:s + 1]),
                         reads=["xs%d" % b], writes=["xn%d" % b, "ss%d" % s])
                    S.op("dve", lambda e, s=s: e.tensor_scalar(out=rs[:, s:s + 1], in0=ss[:, s:s + 1], scalar1=1.0 / D,
                                                               scalar2=EPS, op0=ALU.mult, op1=ALU.add),
                         reads=["ss%d" % s], writes=["rs%d" % s])
                    S.op("act", lambda e, s=s: e.activation(out=rs[:, s:s + 1], in_=rs[:, s:s + 1], func=AF.Sqrt),
                         reads=["rs%d" % s], writes=["rs%d" % s])
                    S.op("dve", lambda e, s=s: e.reciprocal(out=rs[:, s:s + 1], in_=rs[:, s:s + 1]),
                         reads=["rs%d" % s], writes=["rs%d" % s])
                    S.op("dve", lambda e, b=b, s=s: e.scalar_tensor_tensor(
                        out=xs[b][:], in0=xs[b][:], scalar=rs[:, s:s + 1], in1=gt[:], op0=ALU.mult, op1=ALU.mult),
                        reads=["xs%d" % b, "rs%d" % s, "gt"], writes=["xs%d" % b])
                    S.dma(lambda e, b=b, orows=orows: e.dma_start(out=orows, in_=xs[b][:]), reads=["xs%d" % b],
                          writes=["ydst"])
            S.barrier()

        if debug == "prologue":
            pass
        elif debug == "layer0":
            do_layer(0, 2048, 0)
        elif debug == "l0s2":
            do_layer(4096, 4096, 0)
        elif debug == "seq0":
            for l in range(NL):
                do_layer(0, 2048, l)
            do_final(0, 2048)
        else:
            for (t0s, L) in SEQS:
                for l in range(NL):
                    do_layer(t0s, L, l)
                do_final(t0s, L)
        S.barrier()
        S.emit()
    return nc


_CACHE = {}


def _consts():
    c = np.zeros((128, NCONST), np.float32)
    c[:, 0:128] = np.eye(128, dtype=np.float32)
    c[:, CI_K:CI_K + 513] = np.arange(513, dtype=np.float32)[None, :]
    sidx = np.arange(128) // 16
    c[:, CI_EF] = 7 - sidx
    c[:, CI_EB] = sidx
    tidx = (np.arange(128) // 16)[None, :]
    c[:, CI_MF:CI_MF + 128] = (tidx >= sidx[:, None]).astype(np.float32)
    c[:, CI_MB:CI_MB + 128] = (tidx <= sidx[:, None]).astype(np.float32)
    t8 = np.arange(8, dtype=np.float32)
    c[:, CI_EVF:CI_EVF + 16] = np.concatenate([t8 + 1, t8 - 7])[None, :]
    c[:, CI_EVB:CI_EVB + 16] = np.concatenate([8 - t8, -t8])[None, :]
    return c


def make_inputs(x_prompt, x_sample, norm_g, w_in, ln_g, ln_b, w_s, b_s, lam_re, lam_im, log_dt,
           b_re, b_im, c_re, c_im, d_skip, w_glu, b_glu, w_out, final_g):
    f = lambda a: np.ascontiguousarray(np.asarray(a, dtype=np.float32))
    x_prompt, x_sample = f(x_prompt), f(x_sample)
    shared = {
        "w_in": f(w_in), "w_glu": f(w_glu), "w_out": f(w_out),
        "normg": f(np.asarray(norm_g).reshape(NL, 8, 128).transpose(2, 0, 1).reshape(128, 32)),
        "lng": f(ln_g), "lnb": f(ln_b), "bglu": f(b_glu), "finalg": f(np.asarray(final_g).reshape(1, D)),
        "wsT": f(np.asarray(w_s).transpose(0, 3, 1, 2).reshape(NL, 128, 512)),
        "bsP": f(np.asarray(b_s).transpose(0, 2, 1)),
        "lamA_re": f(np.asarray(lam_re).reshape(8, 4096)),
        "lamA_im": f(np.asarray(lam_im).reshape(8, 4096)),
        "ldtA": f(np.repeat(np.asarray(log_dt).reshape(8, 64, 1), 64, axis=2).reshape(8, 4096)),
        "bT_re": f(np.asarray(b_re).reshape(8, 64, 64, 16).transpose(0, 3, 1, 2).reshape(8, 16, 4096)),
        "bT_im": f(np.asarray(b_im).reshape(8, 64, 64, 16).transpose(0, 3, 1, 2).reshape(8, 16, 4096)),
        "lamP_re": f(np.tile(np.asarray(lam_re).reshape(8, 64, 64).transpose(0, 2, 1), (1, 2, 1))),
        "lamP_im": f(np.tile(np.asarray(lam_im).reshape(8, 64, 64).transpose(0, 2, 1), (1, 2, 1))),
        "ldtP": f(np.repeat(np.asarray(log_dt).reshape(8, 1, 64), 128, axis=1)),
        "cP_re": f(np.tile(np.asarray(c_re).reshape(8, 64, 16, 64).transpose(0, 3, 1, 2).reshape(8, 64, 1024), (1, 2, 1))),
        "cP_im": f(np.tile(np.asarray(c_im).reshape(8, 64, 16, 64).transpose(0, 3, 1, 2).reshape(8, 64, 1024), (1, 2, 1))),
        "dskA": f(np.tile(np.asarray(d_skip).reshape(NL, 64, 16).transpose(0, 2, 1), (1, 8, 1))),
        "consts": _consts(),
    }
    in_maps = []
    for c in range(8):
        xc = np.concatenate([x_prompt[2 * c], x_prompt[2 * c + 1], x_sample[2 * c], x_sample[2 * c + 1]], axis=0)
        m = dict(shared)
        m["x"] = np.ascontiguousarray(xc)
        in_maps.append(m)
    return in_maps


def kernel(**inputs):
    in_maps = make_inputs(**inputs)
    if "nc" not in _CACHE:
        _CACHE["nc"] = build_program()
    nc = _CACHE["nc"]
    res = run_bass_kernel_spmd(nc, in_maps, core_ids=list(range(8)))
    yp = np.empty((16, 2048, D), np.float32)
    ysm = np.empty((16, 4096, D), np.float32)
    for c in range(8):
        yc = np.asarray(res.results[c]["y"])
        yp[2 * c] = yc[0:2048]
        yp[2 * c + 1] = yc[2048:4096]
        ysm[2 * c] = yc[4096:8192]
        ysm[2 * c + 1] = yc[8192:12288]
    return (yp, ysm)
```
